# Optimizing a Trainium2 kernel written in Bass

```python
import math
import jax, jax.numpy as jnp
from jax import lax
import numpy as np

D_MODEL = 1024
BATCH = 4
SEQ = 8192
DEPTH = 2

GRID_W = 64
CTX_LEN = 256
NORM_EPS = 1e-6
Q_BLOCK = 128
ROPE_THETA = 10000.0

MLA_HEADS = 6
MLA_NOPE_DIM = 64
MLA_ROPE_DIM = 32
MLA_QK_DIM = MLA_NOPE_DIM + MLA_ROPE_DIM
MLA_V_DIM = 64
MLA_Q_RANK = 384
MLA_KV_RANK = 256
MLA_WIDTH = MLA_HEADS * MLA_V_DIM

SWA_HEADS = 6
SWA_KV_HEADS = 2
SWA_GROUP = SWA_HEADS // SWA_KV_HEADS
SWA_HEAD_DIM = 64
SWA_WINDOW = 128
SWA_BLOCK = 128
SWA_BAND = SWA_BLOCK + 2 * SWA_WINDOW
SWA_WIDTH = SWA_HEADS * SWA_HEAD_DIM

HY_WIDTH = 256
HY_SHORT = 3
HY_BANDS = 8
HY_EMB = 1 + 2 * HY_BANDS
HY_HIDDEN = 64
HY_DECAY_FAST = math.log(1e-2) / 0.3
HY_DECAY_SLOW = math.log(1e-2) / 1.5

D_MIX = MLA_WIDTH + SWA_WIDTH + HY_WIDTH
IN_SIZES = (MLA_Q_RANK, MLA_KV_RANK, MLA_ROPE_DIM, MLA_WIDTH,
            SWA_HEADS * SWA_HEAD_DIM, SWA_KV_HEADS * SWA_HEAD_DIM, SWA_KV_HEADS * SWA_HEAD_DIM, SWA_WIDTH,
            3 * HY_WIDTH, HY_WIDTH)
IN_COLS = sum(IN_SIZES)
IN_SPLITS = tuple(int(s) for s in np.cumsum(IN_SIZES)[:-1])

kernel_name = "hybrid_mla_swa_hyena_dit"


def rmsnorm(x, g):
    xf = x.astype(jnp.float32)
    y = xf * lax.rsqrt(jnp.mean(xf * xf, axis=-1, keepdims=True) + NORM_EPS)
    return (y * g.astype(jnp.float32)).astype(x.dtype)


def axial_rope(num_tokens, rot_dim):
    rows = num_tokens // GRID_W
    row = jnp.repeat(jnp.arange(rows, dtype=jnp.float32), GRID_W)
    col = jnp.tile(jnp.arange(GRID_W, dtype=jnp.float32), rows)
    n_freq = rot_dim // 4
    freqs = ROPE_THETA ** (-jnp.arange(n_freq, dtype=jnp.float32) / n_freq)
    ang = jnp.concatenate([row[:, None] * freqs, col[:, None] * freqs], axis=-1)
    return jnp.cos(ang), jnp.sin(ang)


def apply_rope(x, rope):
    cos, sin = rope
    cos = cos[:, None, :].astype(x.dtype)
    sin = sin[:, None, :].astype(x.dtype)
    x1, x2 = jnp.split(x, 2, axis=-1)
    return jnp.concatenate([x1 * cos - x2 * sin, x2 * cos + x1 * sin], axis=-1)


def modulate_project(x, mod, norm_g, w_in):
    shift, scale, gate = jnp.split(mod, 3, axis=-1)
    h = rmsnorm(x, norm_g) * (1.0 + scale) + shift
    return h @ w_in, gate


def dense_attention(q, k, v, scale):
    s = jnp.einsum("bqhd,bkhd->bhqk", q, k).astype(jnp.float32) * scale
    p = jax.nn.softmax(s, axis=-1).astype(v.dtype)
    return jnp.einsum("bhqk,bkhd->bqhd", p, v)


def blocked_attention(q, k, v, scale):
    B, L, H, dk = q.shape
    qb = q.reshape(B, L // Q_BLOCK, Q_BLOCK, H, dk).transpose(1, 0, 2, 3, 4)
    out = lax.map(lambda qi: dense_attention(qi, k, v, scale), qb)
    return out.transpose(1, 0, 2, 3, 4).reshape(B, L, H, v.shape[-1])


def mla_queries(q_lat, lp, rope):
    B, L, _ = q_lat.shape
    q = (rmsnorm(q_lat, lp["mla_q_norm"]) @ lp["mla_w_uq"]).reshape(B, L, MLA_HEADS, MLA_QK_DIM)
    q_nope, q_rope = jnp.split(q, [MLA_NOPE_DIM], axis=-1)
    if rope is not None:
        q_rope = apply_rope(q_rope, rope)
    return jnp.concatenate([q_nope, q_rope], axis=-1)


def mla_keys_values(kv_lat, k_r, lp, rope):
    B, L, _ = kv_lat.shape
    kv = (rmsnorm(kv_lat, lp["mla_kv_norm"]) @ lp["mla_w_ukv"]).reshape(B, L, MLA_HEADS, MLA_NOPE_DIM + MLA_V_DIM)
    k_nope, v = jnp.split(kv, [MLA_NOPE_DIM], axis=-1)
    k_rope = k_r[:, :, None, :]
    if rope is not None:
        k_rope = apply_rope(k_rope, rope)
    k_rope = jnp.broadcast_to(k_rope, (B, L, MLA_HEADS, MLA_ROPE_DIM))
    return jnp.concatenate([k_nope, k_rope], axis=-1), v


def swa_latent(q, k, v, k_ctx, v_ctx, sink):
    f32 = jnp.float32
    B, L = q.shape[:2]
    nb = L // SWA_BLOCK
    qb = q.reshape(B, nb, SWA_BLOCK, SWA_KV_HEADS, SWA_GROUP, SWA_HEAD_DIM)
    pad = ((0, 0), (SWA_WINDOW, SWA_WINDOW), (0, 0), (0, 0))
    kp, vp = jnp.pad(k, pad), jnp.pad(v, pad)
    idx = jnp.arange(nb)[:, None] * SWA_BLOCK + jnp.arange(SWA_BAND)[None, :]
    kb, vb = kp[:, idx], vp[:, idx]
    kpos = idx - SWA_WINDOW
    qpos = jnp.arange(nb)[:, None] * SWA_BLOCK + jnp.arange(SWA_BLOCK)[None, :]
    diff = kpos[:, None, :] - qpos[:, :, None]
    valid = (jnp.abs(diff) <= SWA_WINDOW) & (kpos[:, None, :] >= 0) & (kpos[:, None, :] < L)
    scale = SWA_HEAD_DIM ** -0.5
    s_loc = jnp.einsum("bnqgrd,bnkgd->bngrqk", qb, kb).astype(f32) * scale
    s_loc = jnp.where(valid[None, :, None, None], s_loc, -jnp.inf)
    s_ctx = jnp.einsum("bnqgrd,bcgd->bngrqc", qb, k_ctx).astype(f32) * scale
    sink_g = sink.astype(f32).reshape(SWA_KV_HEADS, SWA_GROUP)[None, None, :, :, None, None]
    m = jnp.maximum(jnp.maximum(s_loc.max(-1, keepdims=True), s_ctx.max(-1, keepdims=True)), sink_g)
    p_loc = jnp.exp(s_loc - m)
    p_ctx = jnp.exp(s_ctx - m)
    denom = p_loc.sum(-1, keepdims=True) + p_ctx.sum(-1, keepdims=True) + jnp.exp(sink_g - m)
    o = (jnp.einsum("bngrqk,bnkgd->bnqgrd", (p_loc / denom).astype(v.dtype), vb)
         + jnp.einsum("bngrqc,bcgd->bnqgrd", (p_ctx / denom).astype(v.dtype), v_ctx))
    return o.reshape(B, L, SWA_HEADS, SWA_HEAD_DIM)


def swa_context(q, k, v, sink):
    f32 = jnp.float32
    B, C = q.shape[:2]
    qg = q.reshape(B, C, SWA_KV_HEADS, SWA_GROUP, SWA_HEAD_DIM)
    s = jnp.einsum("bqgrd,bkgd->bgrqk", qg, k).astype(f32) * SWA_HEAD_DIM ** -0.5
    sink_col = jnp.broadcast_to(sink.astype(f32).reshape(1, SWA_KV_HEADS, SWA_GROUP, 1, 1), s.shape[:-1] + (1,))
    p = jax.nn.softmax(jnp.concatenate([s, sink_col], axis=-1), axis=-1)[..., :-1]
    o = jnp.einsum("bgrqk,bkgd->bqgrd", p.astype(v.dtype), v)
    return o.reshape(B, C, SWA_HEADS, SWA_HEAD_DIM)


def short_conv(u, w, b):
    up = jnp.pad(u, ((0, 0), (1, 1), (0, 0)))
    return up[:, :-2] * w[0] + up[:, 1:-1] * w[1] + up[:, 2:] * w[2] + b


def hyena_filter(num_tokens, lp):
    f32 = jnp.float32
    t = jnp.linspace(0.0, 1.0, num_tokens, dtype=f32)[:, None]
    w = (2.0 * math.pi / num_tokens) * jnp.arange(num_tokens, dtype=f32)[:, None]
    bands = jnp.linspace(1e-4, HY_BANDS - 1, HY_BANDS, dtype=f32)[None, :]
    z = jnp.concatenate([t, jnp.cos(bands * w), -jnp.sin(bands * w)], axis=-1)
    freq = lp["hy_freq"].astype(f32)
    h = jnp.sin(freq * (z @ lp["hy_w1"].astype(f32) + lp["hy_b1"].astype(f32)))
    h = jnp.sin(freq * (h @ lp["hy_w2"].astype(f32) + lp["hy_b2"].astype(f32)))
    h = h @ lp["hy_w3"].astype(f32) + lp["hy_b3"].astype(f32)
    deltas = jnp.abs(jnp.linspace(HY_DECAY_FAST, HY_DECAY_SLOW, HY_WIDTH, dtype=f32))
    decay = jnp.exp(-t * deltas)
    h = h.reshape(num_tokens, 2, HY_WIDTH) * decay[:, None, :]
    kern = jnp.concatenate([h[:, 0], jnp.zeros((1, HY_WIDTH), f32), h[:0:-1, 1]], axis=0)
    return kern / (jnp.sum(jnp.abs(kern), axis=0, keepdims=True) + NORM_EPS)


def fft_long_conv(z, kern):
    L = z.shape[1]
    zf = jnp.fft.rfft(z.astype(jnp.float32), n=2 * L, axis=1)
    kf = jnp.fft.rfft(kern, n=2 * L, axis=0)
    y = jnp.fft.irfft(zf * kf[None], n=2 * L, axis=1)[:, :L]
    return y.astype(z.dtype)


def hyena_mix(u, lp):
    uc = short_conv(u, lp["hy_conv_w"], lp["hy_conv_b"])
    x0, x1, v = jnp.split(uc, 3, axis=-1)
    kern = hyena_filter(u.shape[1], lp)
    z = v * x1
    return x0 * (fft_long_conv(z, kern) + lp["hy_bias"] * z)


def merge_branches(o_a, o_b, o_h, g_a, g_b, g_h, w_out):
    B, L = o_a.shape[:2]
    y = jnp.concatenate([o_a.reshape(B, L, MLA_WIDTH) * jax.nn.silu(g_a),
                         o_b.reshape(B, L, SWA_WIDTH) * jax.nn.silu(g_b),
                         o_h * jax.nn.silu(g_h)], axis=-1)
    return y @ w_out


def hybrid_layer(x, xc, c, c_ctx, lp, update_ctx):
    B, L, _ = x.shape
    mod_x = (jax.nn.silu(c) @ lp["mod_w"] + lp["mod_b"])[:, None, :]
    mod_c = (jax.nn.silu(c_ctx) @ lp["mod_w"] + lp["mod_b"])[None, None, :]
    px, gate_x = modulate_project(x, mod_x, lp["norm_g"], lp["w_in"])
    pc, gate_c = modulate_project(xc, mod_c, lp["norm_g"], lp["w_in"])
    mq_x, mkv_x, mkr_x, mg_x, sq_x, sk_x, sv_x, sg_x, hu_x, hg_x = jnp.split(px, IN_SPLITS, axis=-1)
    mq_c, mkv_c, mkr_c, mg_c, sq_c, sk_c, sv_c, sg_c, hu_c, hg_c = jnp.split(pc, IN_SPLITS, axis=-1)
    C = xc.shape[1]
    rope_mla = axial_rope(L, MLA_ROPE_DIM)
    rope_swa = axial_rope(L, SWA_HEAD_DIM)

    k_a, v_a = mla_keys_values(mkv_x, mkr_x, lp, rope_mla)
    kc_a, vc_a = mla_keys_values(mkv_c, mkr_c, lp, None)
    q_a = mla_queries(mq_x, lp, rope_mla)
    o_a = blocked_attention(q_a, jnp.concatenate([k_a, kc_a], axis=1),
                            jnp.concatenate([v_a, vc_a], axis=1), MLA_QK_DIM ** -0.5)

    q_b = apply_rope(sq_x.reshape(B, L, SWA_HEADS, SWA_HEAD_DIM), rope_swa)
    k_b = apply_rope(sk_x.reshape(B, L, SWA_KV_HEADS, SWA_HEAD_DIM), rope_swa)
    v_b = sv_x.reshape(B, L, SWA_KV_HEADS, SWA_HEAD_DIM)
    kc_b = sk_c.reshape(B, C, SWA_KV_HEADS, SWA_HEAD_DIM)
    vc_b = sv_c.reshape(B, C, SWA_KV_HEADS, SWA_HEAD_DIM)
    o_b = swa_latent(q_b, k_b, v_b, kc_b, vc_b, lp["swa_sink"])

    o_h = hyena_mix(hu_x, lp)

    x_new = x + gate_x * merge_branches(o_a, o_b, o_h, mg_x, sg_x, hg_x, lp["w_out"])

    if update_ctx:
        oc_a = dense_attention(mla_queries(mq_c, lp, None), kc_a, vc_a, MLA_QK_DIM ** -0.5)
        oc_b = swa_context(sq_c.reshape(B, C, SWA_HEADS, SWA_HEAD_DIM), kc_b, vc_b, lp["swa_sink"])
        oc_h = hyena_mix(hu_c, lp)
        xc = xc + gate_c * merge_branches(oc_a, oc_b, oc_h, mg_c, sg_c, hg_c, lp["w_out"])
    return x_new, xc


def setup_inputs(seed: int = 0) -> dict:
    key = jax.random.key(seed)
    ks = jax.random.split(key, 32)
    f32 = jnp.float32

    def nrm(k, shape, scale):
        return jax.random.normal(k, shape, f32) * scale

    D = D_MODEL
    return {
        "x": nrm(ks[0], (BATCH, SEQ, D), 1.0),
        "c": nrm(ks[1], (BATCH, D), 1.0),
        "ctx": nrm(ks[2], (BATCH, CTX_LEN, D), 1.0),
        "c_ctx": nrm(ks[3], (D,), 1.0),
        "norm_g": 1.0 + nrm(ks[4], (DEPTH, D), 0.05),
        "mod_w": nrm(ks[5], (DEPTH, D, 3 * D), 0.5 * D ** -0.5),
        "mod_b": nrm(ks[6], (DEPTH, 3 * D), 0.02),
        "w_in": nrm(ks[7], (DEPTH, D, IN_COLS), D ** -0.5),
        "mla_q_norm": 1.0 + nrm(ks[8], (DEPTH, MLA_Q_RANK), 0.05),
        "mla_w_uq": nrm(ks[9], (DEPTH, MLA_Q_RANK, MLA_HEADS * MLA_QK_DIM), MLA_Q_RANK ** -0.5),
        "mla_kv_norm": 1.0 + nrm(ks[10], (DEPTH, MLA_KV_RANK), 0.05),
        "mla_w_ukv": nrm(ks[11], (DEPTH, MLA_KV_RANK, MLA_HEADS * (MLA_NOPE_DIM + MLA_V_DIM)), MLA_KV_RANK ** -0.5),
        "swa_sink": nrm(ks[12], (DEPTH, SWA_HEADS), 0.5),
        "hy_conv_w": nrm(ks[13], (DEPTH, HY_SHORT, 3 * HY_WIDTH), HY_SHORT ** -0.5),
        "hy_conv_b": nrm(ks[14], (DEPTH, 3 * HY_WIDTH), 0.02),
        "hy_w1": nrm(ks[15], (DEPTH, HY_EMB, HY_HIDDEN), HY_EMB ** -0.5),
        "hy_b1": nrm(ks[16], (DEPTH, HY_HIDDEN), 0.02),
        "hy_freq": 1.0 + nrm(ks[17], (DEPTH, HY_HIDDEN), 0.05),
        "hy_w2": nrm(ks[18], (DEPTH, HY_HIDDEN, HY_HIDDEN), HY_HIDDEN ** -0.5),
        "hy_b2": nrm(ks[19], (DEPTH, HY_HIDDEN), 0.02),
        "hy_w3": nrm(ks[20], (DEPTH, HY_HIDDEN, 2 * HY_WIDTH), HY_HIDDEN ** -0.5),
        "hy_b3": nrm(ks[21], (DEPTH, 2 * HY_WIDTH), 0.02),
        "hy_bias": nrm(ks[22], (DEPTH, HY_WIDTH), 0.5),
        "w_out": nrm(ks[23], (DEPTH, D_MIX, D), D_MIX ** -0.5),
        "final_norm_g": 1.0 + nrm(ks[24], (D,), 0.05),
    }


def reference(x, c, ctx, c_ctx, norm_g, mod_w, mod_b, w_in, mla_q_norm, mla_w_uq, mla_kv_norm, mla_w_ukv,
              swa_sink, hy_conv_w, hy_conv_b, hy_w1, hy_b1, hy_freq, hy_w2, hy_b2, hy_w3, hy_b3, hy_bias,
              w_out, final_norm_g):
    xc = ctx
    for l in range(DEPTH):
        lp = {
            "norm_g": norm_g[l], "mod_w": mod_w[l], "mod_b": mod_b[l], "w_in": w_in[l],
            "mla_q_norm": mla_q_norm[l], "mla_w_uq": mla_w_uq[l],
            "mla_kv_norm": mla_kv_norm[l], "mla_w_ukv": mla_w_ukv[l],
            "swa_sink": swa_sink[l],
            "hy_conv_w": hy_conv_w[l], "hy_conv_b": hy_conv_b[l],
            "hy_w1": hy_w1[l], "hy_b1": hy_b1[l], "hy_freq": hy_freq[l],
            "hy_w2": hy_w2[l], "hy_b2": hy_b2[l], "hy_w3": hy_w3[l], "hy_b3": hy_b3[l],
            "hy_bias": hy_bias[l], "w_out": w_out[l],
        }
        x, xc = hybrid_layer(x, xc, c, c_ctx, lp, update_ctx=(l < DEPTH - 1))
    return rmsnorm(x, final_norm_g)
```

```python
import contextlib
import math
import numpy as np
import concourse.bass as bass
import concourse.mybir as mybir
from concourse.bass_utils import run_bass_kernel_spmd

F32 = mybir.dt.float32
BF16 = mybir.dt.bfloat16
I32 = mybir.dt.int32
ALU = mybir.AluOpType
AF = mybir.ActivationFunctionType

D = 1024
CTX = 256
EPS = 1e-6
NHA = 6
NHB = 6
HYW = 256

COMPUTE = ("pe", "act", "dve")
QUEUES = ("sp", "sq")
CHUNK = 30000
RING = 16


class Prog:
    def __init__(self, nc):
        self.nc = nc
        self.ops = {e: [] for e in COMPUTE + QUEUES}
        self.lastw = {}
        self.readers = {}
        self.waited = {e: {} for e in COMPUTE + QUEUES}
        self.signals = {e: set() for e in COMPUTE}
        self.bar = {}
        self.pending = set()

    def barrier(self):
        self.bar = {}
        for e in COMPUTE:
            if self.ops[e]:
                self.bar[e] = [len(self.ops[e]) - 1]
        for q in QUEUES:
            n = len(self.ops[q])
            self.bar[q] = list(range(max(0, n - RING), n))
        self.pending = set(COMPUTE + QUEUES)
        self.lastw.clear()
        self.readers.clear()

    def op(self, eng, fn, reads=(), writes=()):
        idx = len(self.ops[eng])
        deps = set()
        bar_deps = set()
        if eng in self.pending:
            self.pending.discard(eng)
            for e, lst in self.bar.items():
                if e == eng and e in QUEUES:
                    continue
                for i in lst:
                    bar_deps.add((e, i))
        for b in reads:
            w = self.lastw.get(b)
            if w is not None:
                deps.add(w)
        for b in writes:
            w = self.lastw.get(b)
            if w is not None:
                deps.add(w)
            for r in self.readers.get(b, ()):
                deps.add(r)
        wd = self.waited[eng]
        best = {}
        final = []
        deps |= bar_deps
        for (pe_, pi_) in deps:
            if pe_ == eng and eng == "pe":
                continue
            if pe_ == eng and pe_ in COMPUTE and (pe_, pi_) not in bar_deps:
                israw = any(self.lastw.get(b) == (pe_, pi_) for b in reads)
                if not israw:
                    continue
            if pe_ in COMPUTE:
                if wd.get(pe_, -1) >= pi_:
                    continue
                best[pe_] = max(best.get(pe_, -1), pi_)
            else:
                slot = (pe_, pi_ % RING)
                val = pi_ // RING + 1
                if wd.get(slot, 0) >= val:
                    continue
                wd[slot] = val
                final.append((pe_, pi_))
        for pe_, pi_ in best.items():
            wd[pe_] = pi_
            self.signals[pe_].add(pi_)
            final.append((pe_, pi_))
        self.ops[eng].append((fn, final))
        me = (eng, idx)
        for b in writes:
            self.lastw[b] = me
            self.readers[b] = []
        for b in reads:
            if b not in writes:
                self.readers.setdefault(b, []).append(me)
        return me

    def emit(self):
        nc = self.nc
        with contextlib.ExitStack() as st:
            sems = {}
            seqno = {}
            for e in COMPUTE:
                sig = sorted(self.signals[e])
                seqno[e] = {i: k for k, i in enumerate(sig)}
                nch = max(1, (len(sig) + CHUNK - 1) // CHUNK)
                sems[e] = [st.enter_context(nc.semaphore(f"c_{e}_{j}")) for j in range(nch)]
            for q in QUEUES:
                sems[q] = [st.enter_context(nc.semaphore(f"q_{q}_{j}")) for j in range(RING)]
            block = st.enter_context(nc.Block())

            def target(pe_, pi_):
                if pe_ in COMPUTE:
                    k = seqno[pe_][pi_]
                    return sems[pe_][k // CHUNK], (k % CHUNK) + 1
                return sems[pe_][pi_ % RING], 16 * (pi_ // RING + 1)

            def run(eng_name, handle):
                for idx, (fn, deps) in enumerate(self.ops[eng_name]):
                    if eng_name in QUEUES and idx >= RING:
                        handle.wait_ge(sems[eng_name][idx % RING], 16 * (idx // RING))
                    for d in deps:
                        s, v = target(*d)
                        handle.wait_ge(s, v)
                    ins = fn(handle)
                    if eng_name in QUEUES:
                        ins.then_inc(sems[eng_name][idx % RING], 16)
                    elif idx in seqno[eng_name]:
                        k = seqno[eng_name][idx]
                        ins.then_inc(sems[eng_name][k // CHUNK], 1)
                if eng_name in QUEUES:
                    n = len(self.ops[eng_name])
                    for slot in range(min(RING, n)):
                        cnt = (n - 1 - slot) // RING + 1
                        handle.wait_ge(sems[eng_name][slot], 16 * cnt)

            @block.sync
            def _(e):
                run("sp", e)

            @block.gpsimd
            def _(e):
                run("sq", e)

            @block.tensor
            def _(e):
                run("pe", e)

            @block.scalar
            def _(e):
                run("act", e)

            @block.vector
            def _(e):
                run("dve", e)


OFF_MQ, OFF_MKV, OFF_MKR, OFF_MG, OFF_SQ, OFF_SK, OFF_SV, OFF_SG, OFF_HU, OFF_HG = (
    0, 384, 640, 672, 1056, 1440, 1568, 1696, 2080, 2848)


class Cfg:
    def __init__(self, L, NS, hs):
        self.L, self.NS, self.hs = L, NS, hs
        self.LK = L + CTX
        if NS == 1:
            self.HA = list(range(6)); self.GB = [0, 1]; self.CH = list(range(256))
        else:
            self.HA = [3 * hs + i for i in range(3)]; self.GB = [hs]
            self.CH = list(range(128 * hs, 128 * hs + 128))
        self.HB = [3 * g + r for g in self.GB for r in range(3)]
        self.nha, self.nhb, self.ngb, self.nch = len(self.HA), len(self.HB), len(self.GB), len(self.CH)
        self.DL = 64 * self.nha + 64 * self.nhb + self.nch
        g = []
        for k in range(3):
            g.append((f"mq{k}", list(range(OFF_MQ + 128 * k, OFF_MQ + 128 * k + 128))))
        for k in range(2):
            g.append((f"mkv{k}", list(range(OFF_MKV + 128 * k, OFF_MKV + 128 * k + 128))))
        g.append(("kr", list(range(OFF_MKR, OFF_MKR + 32))))
        g.append(("krsw", [OFF_MKR + (j + 16) % 32 for j in range(32)]))
        sw64 = [(j + 32) % 64 for j in range(64)]
        self.sq_groups = []
        for i in range(0, self.nhb, 2):
            hh = self.HB[i:i + 2]
            self.sq_groups.append(hh)
            g.append((f"sq{i // 2}", [OFF_SQ + 64 * h + j for h in hh for j in range(64)]))
            g.append((f"sqsw{i // 2}", [OFF_SQ + 64 * h + sw64[j] for h in hh for j in range(64)]))
        g.append(("sk", [OFF_SK + 64 * gg + j for gg in self.GB for j in range(64)]))
        g.append(("sksw", [OFF_SK + 64 * gg + sw64[j] for gg in self.GB for j in range(64)]))
        hu_cols = [OFF_HU + part * 256 + c for part in range(3) for c in self.CH]
        self.nhu = len(hu_cols) // 128
        for k in range(self.nhu):
            g.append((f"hu{k}", hu_cols[128 * k:128 * k + 128]))
        gate_cols = ([OFF_MG + 64 * h + j for h in self.HA for j in range(64)]
                     + [OFF_SG + 64 * h + j for h in self.HB for j in range(64)]
                     + [OFF_HG + c for c in self.CH])
        self.ngt = len(gate_cols) // 128
        for k in range(self.ngt):
            g.append((f"gt{k}", gate_cols[128 * k:128 * k + 128]))
        g.append(("sv", [OFF_SV + 64 * gg + j for gg in self.GB for j in range(64)]))
        self.groups = g
        self.goff = {}
        off = 0
        cols = []
        for name, cc in g:
            self.goff[name] = (off, len(cc))
            off += len(cc)
            cols += cc
        self.cols = np.array(cols)
        self.NC = off
        self.wout_rows = ([64 * h + j for h in self.HA for j in range(64)]
                          + [384 + 64 * h + j for h in self.HB for j in range(64)]
                          + [768 + c for c in self.CH])


def rope_tables(L, rot_dim, grid_w=64):
    rows = L // grid_w
    row = np.repeat(np.arange(rows, dtype=np.float32), grid_w)
    col = np.tile(np.arange(grid_w, dtype=np.float32), rows)
    n_freq = rot_dim // 4
    freqs = (np.float32(10000.0) ** (-np.arange(n_freq, dtype=np.float32) / n_freq)).astype(np.float32)
    ang = np.concatenate([row[:, None] * freqs, col[:, None] * freqs], axis=-1).astype(np.float32)
    cos, sin = np.cos(ang).astype(np.float32), np.sin(ang).astype(np.float32)
    cs = np.concatenate([cos, cos], axis=1).T
    sn = np.concatenate([-sin, sin], axis=1).T
    cs = np.concatenate([cs, np.ones((rot_dim, CTX), np.float32)], axis=1)
    sn = np.concatenate([sn, np.zeros((rot_dim, CTX), np.float32)], axis=1)
    return np.ascontiguousarray(cs, np.float32), np.ascontiguousarray(sn, np.float32)


class Builder:
    def __init__(self, L, NS, nlayers=2, debug=False, phases="ABCDE", half_last=False):
        self.L, self.NS, self.nlayers, self.debug, self.phases = L, NS, nlayers, debug, phases
        self.half_last = half_last
        self.LQ = L // 2 if half_last else L
        self.cfg = Cfg(L, NS, 0)
        self.LK = L + CTX
        self.nc = bass.Bass("TRN2", target_bir_lowering=False)
        self.P = Prog(self.nc)
        self.st = contextlib.ExitStack()
        self.inputs = {}
        self.cnt = 0

    def din(self, name, shape, dt=F32):
        t = self.nc.dram_tensor(name, list(shape), dt, kind="ExternalInput")
        self.inputs[name] = (tuple(shape), dt)
        return t.ap()

    def dscr(self, name, shape, dt, out=False):
        kind = "ExternalOutput" if (out or self.debug) else "Internal"
        return self.nc.dram_tensor(name, list(shape), dt, kind=kind).ap()

    def sb(self, name, shape, dt):
        return self.st.enter_context(self.nc.sbuf_tensor(name, list(shape), dt))

    def psum(self, name, shape, dt):
        return self.st.enter_context(self.nc.psum_tensor(name, list(shape), dt))

    def ld(self, out, in_, reads=(), writes=()):
        self.P.op("sp", lambda e: e.dma_start(out=out, in_=in_), reads, writes)

    def stq(self, out, in_, reads=(), writes=()):
        self.P.op("sq", lambda e: e.dma_start(out=out, in_=in_), reads, writes)

    def mm(self, out, lhsT, rhs, start, stop, reads, writes):
        self.P.op("pe", lambda e: e.matmul(out, lhsT=lhsT, rhs=rhs, start=start, stop=stop), reads, writes)

    def tr(self, out, in_, ident, reads, writes):
        self.P.op("pe", lambda e: e.transpose(out=out, in_=in_, identity=ident), reads, writes)

    def act(self, out, in_, func, reads, writes, scale=1.0, bias=None, accum=None):
        def f(e):
            kw = {}
            if bias is not None:
                kw["bias"] = bias
            if accum is not None:
                kw["accum_out"] = accum
            return e.activation(out=out, in_=in_, func=func, scale=scale, **kw)
        self.P.op("act", f, reads, writes)

    def tt(self, out, in0, in1, op, reads, writes, eng="dve"):
        self.P.op(eng, lambda e: e.tensor_tensor(out=out, in0=in0, in1=in1, op=op), reads, writes)

    def ts(self, out, in0, s1, op0, reads, writes, s2=None, op1=None, eng="dve"):
        def f(e):
            if op1 is None:
                return e.tensor_scalar(out=out, in0=in0, scalar1=s1, scalar2=None, op0=op0)
            return e.tensor_scalar(out=out, in0=in0, scalar1=s1, scalar2=s2, op0=op0, op1=op1)
        self.P.op(eng, f, reads, writes)

    def stt(self, out, in0, scalar, in1, op0, op1, reads, writes):
        self.P.op("dve", lambda e: e.scalar_tensor_tensor(out=out, in0=in0, scalar=scalar, in1=in1, op0=op0, op1=op1), reads, writes)

    def cp(self, out, in_, reads, writes, eng="dve"):
        if eng == "act":
            self.P.op("act", lambda e: e.copy(out=out, in_=in_), reads, writes)
        else:
            self.P.op("dve", lambda e: e.tensor_copy(out=out, in_=in_), reads, writes)

    def memset(self, ap, val, writes):
        self.P.op("dve", lambda e: e.memset(ap, val), (), writes)

    def recip(self, out, in_, reads, writes):
        self.P.op("dve", lambda e: e.reciprocal(out=out, in_=in_), reads, writes)

    def rot(self, pool):
        tiles, state = pool
        i = state[0] % len(tiles)
        state[0] += 1
        return tiles[i]


class Arena:
    def __init__(self, base_ap, nwords):
        self.base, self.nwords, self.off = base_ap, nwords, 0

    def mark(self):
        return self.off

    def reset(self, off=0):
        self.off = off

    def alloc(self, shape, dt):
        esz = 4 if dt in (F32, I32) else 2
        nfree = int(np.prod(shape[1:]))
        words = (nfree * esz + 3) // 4
        words = (words + 7) // 8 * 8
        assert self.off + words <= self.nwords, f"arena overflow {self.off}+{words}>{self.nwords}"
        v = self.base[:, self.off:self.off + words]
        self.off += words
        if dt != F32:
            v = v.bitcast(dt)
        v = v[:, 0:nfree]
        if len(shape) == 3:
            v = v.rearrange("p (a b) -> p a b", a=shape[1])
        elif len(shape) == 4:
            v = v.rearrange("p (a b c) -> p a b c", a=shape[1], b=shape[2])
        if shape[0] < 128:
            v = v[0:shape[0]]
        return v


def _build(self):
    cfg, L, LK, NS, P = self.cfg, self.L, self.LK, self.NS, self.P
    nc = self.nc
    nl = self.nlayers
    nha, nhb, ngb, DL = cfg.nha, cfg.nhb, cfg.ngb, cfg.DL
    NC = cfg.NC
    NCH = L // 512
    x_in = self.din("x", [L, D]); ctx_in = self.din("ctx", [CTX, D])
    cc_in = self.din("cc", [128, 16])
    ident_in = self.din("ident", [128, 128])
    csa_in = self.din("cs_a", [32, LK]); sna_in = self.din("sn_a", [32, LK])
    csb_in = self.din("cs_b", [64, LK]); snb_in = self.din("sn_b", [64, LK])
    normg_in = self.din("norm_g", [nl, D]); modw_in = self.din("mod_w", [nl, D, 3 * D])
    modb_in = self.din("mod_b", [nl, 3 * D]); win_in = self.din("w_in", [nl, D, NC])
    qn_in = self.din("qn", [nl, 128, 3]); wuq_in = self.din("w_uq", [nl, 384, nha * 128])
    kvn_in = self.din("kvn", [nl, 128, 2]); wkk_in = self.din("w_ukvk", [nl, 256, nha * 64])
    wkv_in = self.din("w_ukvv", [nl, 256, nha * 64])
    sink_in = self.din("sink", [nl, nhb])
    wout_in = self.din("w_out", [nl, D, D]); fng_in = self.din("fng", [D])
    nch_, ncg_ = cfg.nch, cfg.nch // 128
    HW = dict(cw=self.din("hy_cw", [nl, 128, 3 * ncg_, 3]), cb=self.din("hy_cb", [nl, 128, 3 * ncg_]),
              hbias=self.din("hy_hbias", [nl, 128, ncg_]), w1=self.din("hy_w1", [nl, 17, 64]), w2=self.din("hy_w2", [nl, 64, 64]),
              w3=self.din("hy_w3", [nl, 64, 2 * nch_]), b1=self.din("hy_b1", [nl, 64, 1]), b2=self.din("hy_b2", [nl, 64, 1]),
              fr=self.din("hy_fr", [nl, 64, 1]), b3=self.din("hy_b3", [nl, 128, 2 * ncg_]), ndl=self.din("hy_ndl", [128, ncg_]))
    for sn_, Ls_ in (("x", L), ("c", CTX)):
        Tt = fft_tables(Ls_)
        Td = dict(Tt)
        for nm in ("F1", "TT1", "TT2", "BDC", "BDS", "BDSn", "TI1", "TI2", "IC", "ISn"):
            Td[nm] = self.din(f"ft{sn_}_{nm}", Tt[nm].shape)
        HW["T" + sn_] = Td
        HW["zemb" + sn_] = self.din(f"zemb{sn_}", [17, Ls_]); HW["tlin" + sn_] = self.din(f"tlin{sn_}", [1, Ls_])
    mnext_in = self.din("mnext", [128, 128]); mprev_in = self.din("mprev", [128, 128])
    out_ap = self.dscr("out", [self.LQ, D], F32, out=True)
    HW["flag"] = self.din("hy_flag", [128, 1])

    QT = self.dscr("QT", [nha, 96, LK], BF16); KR = self.dscr("KR", [32, LK], BF16)
    KN = self.dscr("KN", [nha, 64, LK], BF16); VD = self.dscr("VD", [LK, nha * 64], BF16)
    SQT = self.dscr("SQT", [nhb, 64, LK], BF16); SKT = self.dscr("SKT", [ngb, 64, LK], BF16)
    SVD = self.dscr("SVD", [LK, ngb * 64], BF16)
    UT = self.dscr("UT", [cfg.nhu * 128, LK], BF16)
    GT = self.dscr("GT", [DL, LK], BF16)
    YT = self.dscr("YT", [DL, LK], BF16)
    X1 = self.dscr("X1", [L, D], F32); XC1 = self.dscr("XC1", [CTX, D], F32)
    HND = self.dscr("HND", [2, cfg.nch, LK], BF16); KFD = self.dscr("KFD", [cfg.nch // 2, 128, 4 * (L // 32)], F32)
    ZD = self.dscr("ZD", [cfg.nch, LK], BF16); YC = self.dscr("YC", [cfg.nch, LK], F32)
    self.scr = dict(QT=QT, KR=KR, KN=KN, VD=VD, SQT=SQT, SKT=SKT, SVD=SVD, UT=UT, GT=GT, YT=YT, X1=X1, XC1=XC1, HND=HND, KFD=KFD, ZD=ZD, YC=YC)

    ARW = 46 * 1024
    arena_t = self.sb("arena", [128, ARW], F32)
    AR = Arena(arena_t[:], ARW)
    S0 = self.psum("S0", [128, 1024], F32); S1 = self.psum("S1", [128, 1024], F32)
    B4 = self.psum("B4", [128, 512], F32); B5 = self.psum("B5", [128, 512], F32)
    B6 = self.psum("B6", [128, 512], F32); B7 = self.psum("B7", [128, 512], F32)
    banks = [(S0[:, 0:512], "S0a"), (S0[:, 512:1024], "S0b"), (S1[:, 0:512], "S1a"), (S1[:, 512:1024], "S1b"),
             (B4[:], "B4"), (B5[:], "B5")]
    bstate = [0]

    def bank():
        b = banks[bstate[0] % len(banks)]
        bstate[0] += 1
        return b
    T6 = B6[:].bitcast(BF16); T7 = B7[:].bitcast(BF16)
    tbanks = [(T6, "B6"), (T7, "B7")]
    tstate = [0]

    def tbank():
        b = tbanks[tstate[0] % 2]
        tstate[0] += 1
        return b

    ident = AR.alloc([128, 128], BF16); ones = AR.alloc([128, 128], BF16)
    MODD = self.dscr("MODD", [6, 128, D], F32)
    xt = [AR.alloc([128, D], F32) for _ in range(3)]
    xst = [0]

    def xtile():
        i = xst[0] % 3
        xst[0] += 1
        return xt[i], f"xt{i}"
    base_mark = AR.mark()

    t_, k_ = xtile()
    self.ld(t_[:, 0:128], ident_in, writes=[k_])
    self.cp(ident, t_[:, 0:128], [k_], ["ident"])
    self.memset(ones, 1.0, ["ones"])

    for l in range(nl):
        last = (l == nl - 1)
        x_src = x_in if l == 0 else X1
        c_src = ctx_in if l == 0 else XC1
        P.barrier()
        AR.reset(base_mark)
        Amod = [AR.alloc([128, D], F32) for _ in range(2)]
        Smod = [AR.alloc([128, D], F32) for _ in range(2)]
        Gmod = [AR.alloc([128, D], F32) for _ in range(2)]
        sc = AR.alloc([128, 16], F32); screp = AR.alloc([128, 16, 128], F32)
        modb = AR.alloc([128, 3 * D], F32); ng = AR.alloc([128, D], F32)
        mws = [AR.alloc([128, 8, 512], F32) for _ in range(2)]
        self.ld(sc, cc_in, writes=["sc"])
        self.act(sc, sc, AF.Silu, ["sc"], ["sc"])
        for i in range(16):
            self.cp(screp[:, i, :], sc[:, i:i + 1].to_broadcast([128, 128]), ["sc"], ["screp"])
        self.ld(modb, modb_in[l:l + 1, :].to_broadcast([128, 3 * D]), writes=["modb"])
        self.ld(ng, normg_in[l:l + 1, :].to_broadcast([128, D]), writes=["ng"])
        mw_v = modw_in[l].rearrange("(k p) n -> p k n", p=128)
        for n in range(6):
            mwt = mws[n % 2]; mk = f"mws{n % 2}"
            self.ld(mwt, mw_v[:, :, n * 512:(n + 1) * 512], writes=[mk])
            for j in range(2):
                ps, pk = bank()
                for k in range(8):
                    self.mm(ps, screp[:, j * 8 + k, :], mwt[:, k, :], k == 0, k == 7, ["screp", mk], [pk])
                cs_ = slice((n % 2) * 512, (n % 2) * 512 + 512)
                mb = modb[:, n * 512:(n + 1) * 512]
                if n < 2:
                    self.tt(Smod[j][:, cs_], ps, mb, ALU.add, [pk, "modb"], [f"Smod{j}"])
                elif n < 4:
                    self.stt(Amod[j][:, cs_], ps, 1.0, mb, ALU.add, ALU.add, [pk, "modb"], [f"Amod{j}"])
                    self.tt(Amod[j][:, cs_], Amod[j][:, cs_], ng[:, cs_], ALU.mult, [f"Amod{j}", "ng"], [f"Amod{j}"])
                else:
                    self.tt(Gmod[j][:, cs_], ps, mb, ALU.add, [pk, "modb"], [f"Gmod{j}"])
        for j in range(2):
            self.stq(MODD[j], Amod[j], [f"Amod{j}"], ["MODD"])
            self.stq(MODD[2 + j], Smod[j], [f"Smod{j}"], ["MODD"])
            self.stq(MODD[4 + j], Gmod[j], [f"Gmod{j}"], ["MODD"])
        if self.debug and l == 0:
            dbg = self.dscr("dbg_mod", [3, 128, D], F32)
            self.stq(dbg[0], Amod[0], ["Amod0"]); self.stq(dbg[1], Smod[0], ["Smod0"]); self.stq(dbg[2], Gmod[1], ["Gmod1"])

        if "A" in self.phases:
            P.barrier()
            AR.reset(base_mark)
            self.phaseA(l, AR, x_src, c_src, bank, tbank, xtile, ident, ones, MODD,
                        dict(win=win_in, qn=qn_in, wuq=wuq_in, kvn=kvn_in, wkk=wkk_in, wkv=wkv_in,
                             csa=csa_in, sna=sna_in, csb=csb_in, snb=snb_in))
        if "B" in self.phases and "D" in self.phases:
            P.barrier()
            AR.reset(base_mark)
            H = self.hy_setup(l, AR, HW, xtile)
            self.hy_pre(H, AR, S0[:], S1[:], HW)
            P.barrier()
            AR.reset(H["mark"])
            gB = self.genB(l, AR, S0[:], S1[:], (B4[:], "B4"))
            next(gB)
            gD = self.hy_fft_gen(H, AR, (B5[:], "B5"), (B6[:], "B6"), (B7[:], "B7"))
            next(gD)
            LQb = self.LQ if last else L
            nB = nha * ((LQb // 512) * (LK // 256) + (1 if not last else 0))
            nB = int(nB * 0.92)
            nDx = 3 * (cfg.nch // 2) + 8
            costD = nDx * (1.0 if last else 1.12)
            ratio = costD / nB
            acc = 0.0
            aliveB, aliveD = True, True
            nxt = 1.0
            while aliveB:
                try:
                    next(gB)
                except StopIteration:
                    aliveB = False
                acc += ratio
                while aliveD and acc >= nxt:
                    acc -= nxt
                    try:
                        nxt = next(gD) or 1.0
                    except StopIteration:
                        aliveD = False
            while aliveD:
                try:
                    next(gD)
                except StopIteration:
                    aliveD = False
            P.barrier()
            AR.reset(H["mark"])
            self.hy_post(H, AR)
        elif "B" in self.phases:
            P.barrier()
            AR.reset(base_mark)
            self.phaseB(l, AR, S0[:], S1[:], B4[:], B5[:])
        if "C" in self.phases:
            P.barrier()
            AR.reset(base_mark)
            self.phaseC(l, AR, S0[:], S1[:], B4[:], B5[:], sink_in, mnext_in, mprev_in, xtile)
        zr = []
        if "B" not in self.phases:
            zr.append((0, 64 * nha))
        if "C" not in self.phases:
            zr.append((64 * nha, 64 * nha + 64 * nhb))
        if "D" not in self.phases:
            zr.append((64 * nha + 64 * nhb, DL))
        if zr and "E" in self.phases:
            P.barrier()
            AR.reset(base_mark)
            zt = AR.alloc([128, 512], BF16)
            self.memset(zt, 0.0, ["zt"])
            for (r0, r1) in zr:
                for r in range(r0, r1, 64):
                    for t0 in range(0, LK, 512):
                        n = min(512, LK - t0)
                        self.stq(YT[r:r + 64, t0:t0 + n], zt[0:64, 0:n], ["zt"], [f"YTz"])
        if "E" in self.phases:
            P.barrier()
            AR.reset(base_mark)
            self.phaseE(l, AR, x_src, c_src, bank, xtile, MODD, wout_in, fng_in, out_ap, YT)
    P.barrier()
    if "E" not in self.phases:
        t_, k_ = xtile()
        self.memset(t_, 0.0, [k_])
        for i in range(self.LQ // 128):
            self.stq(out_ap[i * 128:(i + 1) * 128, :], t_, [k_])
    P.emit()
    return nc


Builder.build = _build


def _phaseA(self, l, AR, x_src, c_src, bank, tbank, xtile, ident, ones, MODD, W):
    cfg, L, LK, P = self.cfg, self.L, self.LK, self.P
    nha, nhb, ngb, NC = cfg.nha, cfg.nhb, cfg.ngb, cfg.NC
    scr = self.scr
    update_ctx = (l < self.nlayers - 1)
    Amod = [AR.alloc([128, D], F32) for _ in range(2)]
    Smod = [AR.alloc([128, D], F32) for _ in range(2)]
    for j in range(2):
        self.ld(Amod[j], MODD[j], ["MODD"], [f"Amod{j}"])
        self.ld(Smod[j], MODD[2 + j], ["MODD"], [f"Smod{j}"])
    wbf = AR.alloc([128, 8, NC], BF16)
    wuq = AR.alloc([128, 3, nha * 128], BF16)
    wk = AR.alloc([128, 2, nha * 96], BF16)
    wv = AR.alloc([128, 2, nha * 64], BF16)
    qn = AR.alloc([128, 3], F32); kvn = AR.alloc([128, 2], F32)
    eps_t = AR.alloc([128, 1], F32)
    self.memset(eps_t, EPS, ["eps"])
    self.ld(qn, W["qn"][l], writes=["qn"]); self.ld(kvn, W["kvn"][l], writes=["kvn"])
    win_v = W["win"][l].rearrange("(k p) n -> p k n", p=128)
    gi = 0
    for name, cols in cfg.groups:
        off, n = cfg.goff[name]
        t_, k_ = xtile()
        tv = t_[:, 0:8 * n].rearrange("p (k n) -> p k n", k=8)
        self.ld(tv, win_v[:, :, off:off + n], writes=[k_])
        sc_ = 0.125 if name.startswith("sq") else 1.0
        if gi % 2 == 0:
            self.ts(wbf[:, :, off:off + n], tv, sc_, ALU.mult, [k_], [f"wbf_{name}"])
        else:
            self.act(wbf[:, :, off:off + n], tv, AF.Copy, [k_], [f"wbf_{name}"], scale=sc_)
        gi += 1
    wuq_v = W["wuq"][l].rearrange("(k p) n -> p k n", p=128)
    for k in range(3):
        t_, k_ = xtile()
        self.ld(t_[:, 0:nha * 128], wuq_v[:, k, :], writes=[k_])
        self.ts(wuq[:, k, :], t_[:, 0:nha * 128], qn[:, k:k + 1], ALU.mult, [k_, "qn"], ["wuq"],
                s2=float(96 ** -0.5), op1=ALU.mult)
    self.memset(wk, 0.0, ["wk"])
    wkk_v = W["wkk"][l].rearrange("(k p) n -> p k n", p=128)
    wkv_v = W["wkv"][l].rearrange("(k p) n -> p k n", p=128)
    for k in range(2):
        t_, k_ = xtile()
        self.ld(t_[:, 0:nha * 64], wkk_v[:, k, :], writes=[k_])
        self.ts(wk[:, k, :].rearrange("p (h c) -> p h c", h=nha)[:, :, 0:64],
                t_[:, 0:nha * 64].rearrange("p (h c) -> p h c", h=nha), kvn[:, k:k + 1], ALU.mult,
                [k_, "kvn"], ["wk"])
        t_, k_ = xtile()
        self.ld(t_[:, 0:nha * 64], wkv_v[:, k, :], writes=[k_])
        self.ts(wv[:, k, :], t_[:, 0:nha * 64], kvn[:, k:k + 1], ALU.mult, [k_, "kvn"], ["wv"])
    hb = AR.alloc([128, 4, D], BF16)
    junk = AR.alloc([128, D], BF16)
    tmpf = AR.alloc([128, D], F32)
    hT = AR.alloc([128, 8, 512], BF16)
    qlT = AR.alloc([128, 3, 512], BF16); sqq = AR.alloc([128, 3, 512], BF16)
    kvlT = AR.alloc([128, 2, 512], BF16); sqkv = AR.alloc([128, 2, 512], BF16)
    csa = AR.alloc([128, 512], F32); sna = AR.alloc([128, 512], F32)
    csb = AR.alloc([128, 512], F32); snb = AR.alloc([128, 512], F32)
    rq = AR.alloc([128, 512], F32); rkv = AR.alloc([128, 512], F32)
    rca = AR.alloc([128, 512], F32); rsa = AR.alloc([128, 512], F32)
    ss = AR.alloc([128, 4], F32); rstd = AR.alloc([128, 4], F32); rtok = AR.alloc([128, 4], F32)
    NOF, NOB = 4, 8
    of = ([AR.alloc([128, 512], F32) for _ in range(NOF)], [0])
    ob = ([AR.alloc([128, 512], BF16) for _ in range(NOB)], [0])

    def rotf():
        i = of[1][0] % NOF; of[1][0] += 1
        return of[0][i], f"of{i}"

    def rotb():
        i = ob[1][0] % NOB; ob[1][0] += 1
        return ob[0][i], f"ob{i}"
    ev = [0]

    chunks = [(c * 512, 512, 0) for c in range(L // 512)] + [(L, CTX, 1)]
    lastl = (l == self.nlayers - 1)

    def make_chunk(t0, n, j):
        isctx = (j == 1)
        qside = ((not isctx) and ((not lastl) or t0 < self.LQ)) or (isctx and update_ctx)
        nt = n // 128
        src = c_src if isctx else x_src
        s0 = 0 if isctx else t0
        hTk = [f"hT{i}" for i in range(nt)]

        def proj(name, pb=0):
            off, m = cfg.goff[name]
            ps, pk = bank()
            for k in range(8):
                self.mm(ps[pb:pb + m, 0:n], wbf[:, k, off:off + m], hT[:, k, 0:n], k == 0, k == 7,
                        [f"wbf_{name}"] + hTk, [pk])
            return ps, pk

        def bsum(sq_t, key, nk, dst, dkey, width):
            ps, pk = bank()
            for k in range(nk):
                self.mm(ps[:, 0:n], ones, sq_t[:, k, 0:n], k == 0, k == nk - 1, ["ones", key], [pk])
            self.ts(dst[:, 0:n], ps[:, 0:n], 1.0 / width, ALU.mult, [pk], [dkey], s2=EPS, op1=ALU.add)
            self.act(dst[:, 0:n], dst[:, 0:n], AF.Ln, [dkey], [dkey])
            self.act(dst[:, 0:n], dst[:, 0:n], AF.Exp, [dkey], [dkey], scale=-0.5)


        def norm_():
            for i in range(nt):
                xt_, xk = xtile()
                self.ld(xt_, src[s0 + i * 128:s0 + (i + 1) * 128, :], writes=[xk])
                self.act(junk, xt_, AF.Square, [xk], ["junk", "ss"], accum=ss[:, i:i + 1])
                self.ts(rstd[:, i:i + 1], ss[:, i:i + 1], 1.0 / D, ALU.mult, ["ss"], ["rstd"], s2=EPS, op1=ALU.add)
                self.act(rstd[:, i:i + 1], rstd[:, i:i + 1], AF.Sqrt, ["rstd"], ["rstd"])
                self.recip(rstd[:, i:i + 1], rstd[:, i:i + 1], ["rstd"], ["rstd"])
                self.stt(tmpf, xt_, rstd[:, i:i + 1], Amod[j], ALU.mult, ALU.mult, [xk, "rstd", f"Amod{j}"], ["tmpf"])
                self.tt(hb[:, i, :], tmpf, Smod[j], ALU.add, ["tmpf", f"Smod{j}"], [f"hb{i}"])
                tb, tk = tbank()
                for k in range(8):
                    self.tr(tb[:, k * 128:(k + 1) * 128], hb[:, i, k * 128:(k + 1) * 128], ident, [f"hb{i}", "ident"], [tk])
                self.cp(hT[:, :, i * 128:(i + 1) * 128], tb.rearrange("p (k t) -> p k t", k=8), [tk], [f"hT{i}"],
                        eng="act" if i % 2 else "dve")

        def first_():
            self.ld(csa[64:96, 0:n], W["csa"][:, t0:t0 + n], writes=["csa"])
            self.ld(sna[64:96, 0:n], W["sna"][:, t0:t0 + n], writes=["sna"])
            for hh in range(2):
                self.ld(csb[64 * hh:64 * hh + 64, 0:n], W["csb"][:, t0:t0 + n], writes=["csb"])
                self.ld(snb[64 * hh:64 * hh + 64, 0:n], W["snb"][:, t0:t0 + n], writes=["snb"])
            for k in range(3 if qside else 0):
                ps, pk = proj(f"mq{k}")
                self.act(qlT[:, k, 0:n], ps[:, 0:n], AF.Copy, [pk], ["qlT"])
                self.act(sqq[:, k, 0:n], ps[:, 0:n], AF.Square, [pk], ["sqq"])
            for k in range(2):
                ps, pk = proj(f"mkv{k}")
                self.act(kvlT[:, k, 0:n], ps[:, 0:n], AF.Copy, [pk], ["kvlT"])
                self.act(sqkv[:, k, 0:n], ps[:, 0:n], AF.Square, [pk], ["sqkv"])
            ps, pk = proj("kr", 64); ps2, pk2 = proj("krsw", 64)
            fa, fak = rotf(); fb, fbk = rotf(); ko, kok = rotb()
            self.tt(fa[64:96, 0:n], ps[64:96, 0:n], csa[64:96, 0:n], ALU.mult, [pk, "csa"], [fak])
            self.tt(fb[64:96, 0:n], ps2[64:96, 0:n], sna[64:96, 0:n], ALU.mult, [pk2, "sna"], [fbk])
            self.tt(ko[64:96, 0:n], fa[64:96, 0:n], fb[64:96, 0:n], ALU.add, [fak, fbk], [kok])
            self.stq(scr["KR"][:, t0:t0 + n], ko[64:96, 0:n], [kok])
            def rope_pair(nm, nmsw, heads, dst):
                m = 64 * len(heads)
                ps, pk = proj(nm); ps2, pk2 = proj(nmsw)
                fa, fak = rotf(); fb, fbk = rotf(); qo, qok = rotb()
                self.tt(fa[0:m, 0:n], ps[0:m, 0:n], csb[0:m, 0:n], ALU.mult, [pk, "csb"], [fak])
                self.tt(fb[0:m, 0:n], ps2[0:m, 0:n], snb[0:m, 0:n], ALU.mult, [pk2, "snb"], [fbk])
                self.tt(qo[0:m, 0:n], fa[0:m, 0:n], fb[0:m, 0:n], ALU.add, [fak, fbk], [qok])
                for ii, hidx in enumerate(heads):
                    self.stq(dst[hidx, :, t0:t0 + n], qo[64 * ii:64 * ii + 64, 0:n], [qok])
            if qside:
                for gi_, hh in enumerate(cfg.sq_groups):
                    rope_pair(f"sq{gi_}", f"sqsw{gi_}", [cfg.HB.index(h) for h in hh], scr["SQT"])
            rope_pair("sk", "sksw", list(range(ngb)), scr["SKT"])
            off, m = cfg.goff["sv"]
            for i in range(nt):
                ps, pk = bank()
                for k in range(8):
                    self.mm(ps[:, 0:m], hT[:, k, i * 128:(i + 1) * 128], wbf[:, k, off:off + m], k == 0, k == 7, [f"hT{i}", "wbf_sv"], [pk])
                vo, vok = rotb()
                self.act(vo[:, 0:m], ps[:, 0:m], AF.Copy, [pk], [vok])
                self.stq(scr["SVD"][t0 + i * 128:t0 + (i + 1) * 128, :], vo[:, 0:m], [vok])
            if (not isctx) or update_ctx:
                for k in range(cfg.nhu):
                    if (not qside) and k < cfg.nhu // 3 and not (lastl and (not isctx) and t0 < self.LQ + 512):
                        continue
                    ps, pk = proj(f"hu{k}")
                    uo, uok = rotb()
                    self.act(uo[:, 0:n], ps[:, 0:n], AF.Copy, [pk], [uok])
                    self.stq(scr["UT"][k * 128:(k + 1) * 128, t0:t0 + n], uo[:, 0:n], [uok])
                for k in range(cfg.ngt if qside else 0):
                    ps, pk = proj(f"gt{k}")
                    go, gok = rotb()
                    self.act(go[:, 0:n], ps[:, 0:n], AF.Silu, [pk], [gok])
                    self.stq(scr["GT"][k * 128:(k + 1) * 128, t0:t0 + n], go[:, 0:n], [gok])


        def second_():
            if qside:
                bsum(sqq, "sqq", 3, rq, "rq", 384)
                self.tt(rca[64:96, 0:n], rq[64:96, 0:n], csa[64:96, 0:n], ALU.mult, ["rq", "csa"], ["rca"])
                self.tt(rsa[64:96, 0:n], rq[64:96, 0:n], sna[64:96, 0:n], ALU.mult, ["rq", "sna"], ["rsa"])
            if qside:
                for h in range(nha):
                    ps, pk = bank()
                    for k in range(3):
                        self.mm(ps[0:96, 0:n], wuq[:, k, h * 128:h * 128 + 96], qlT[:, k, 0:n], k == 0, k == 2, ["wuq", "qlT"], [pk])
                    ps2, pk2 = bank()
                    for k in range(3):
                        self.mm(ps2[64:96, 0:n], wuq[:, k, h * 128 + 96:h * 128 + 128], qlT[:, k, 0:n], k == 0, k == 2, ["wuq", "qlT"], [pk2])
                    fa, fak = rotf(); fb, fbk = rotf(); qo, qok = rotb()
                    self.tt(fa[64:96, 0:n], ps[64:96, 0:n], rca[64:96, 0:n], ALU.mult, [pk, "rca"], [fak])
                    self.tt(fb[64:96, 0:n], ps2[64:96, 0:n], rsa[64:96, 0:n], ALU.mult, [pk2, "rsa"], [fbk])
                    self.tt(qo[64:96, 0:n], fa[64:96, 0:n], fb[64:96, 0:n], ALU.add, [fak, fbk], [qok])
                    self.tt(qo[0:64, 0:n], ps[0:64, 0:n], rq[0:64, 0:n], ALU.mult, [pk, "rq"], [qok])
                    self.stq(scr["QT"][h, :, t0:t0 + n], qo[0:96, 0:n], [qok])
            bsum(sqkv, "sqkv", 2, rkv, "rkv", 256)
            for h in range(nha):
                ps, pk = bank()
                for k in range(2):
                    self.mm(ps[0:96, 0:n], wk[:, k, h * 96:(h + 1) * 96], kvlT[:, k, 0:n], k == 0, k == 1, ["wk", "kvlT"], [pk])
                ko, kok = rotb()
                self.tt(ko[0:64, 0:n], ps[0:64, 0:n], rkv[0:64, 0:n], ALU.mult, [pk, "rkv"], [kok])
                self.stq(scr["KN"][h, :, t0:t0 + n], ko[0:64, 0:n], [kok])
            ps, pk = bank()
            for i in range(nt):
                for k in range(2):
                    self.mm(ps[:, i:i + 1], sqkv[:, k, i * 128:(i + 1) * 128], ones[:, 0:1], k == 0, k == 1, ["sqkv", "ones"], [pk])
            self.ts(rtok[:, 0:nt], ps[:, 0:nt], 1.0 / 256, ALU.mult, [pk], ["rtok"], s2=EPS, op1=ALU.add)
            self.act(rtok[:, 0:nt], rtok[:, 0:nt], AF.Sqrt, ["rtok"], ["rtok"])
            self.recip(rtok[:, 0:nt], rtok[:, 0:nt], ["rtok"], ["rtok"])
            for i in range(nt):
                ps, pk = bank()
                for k in range(2):
                    self.mm(ps[:, 0:nha * 64], kvlT[:, k, i * 128:(i + 1) * 128], wv[:, k, :], k == 0, k == 1, ["kvlT", "wv"], [pk])
                vo, vok = rotb()
                self.ts(vo[:, 0:nha * 64], ps[:, 0:nha * 64], rtok[:, i:i + 1], ALU.mult, [pk, "rtok"], [vok])
                self.stq(scr["VD"][t0 + i * 128:t0 + (i + 1) * 128, :], vo[:, 0:nha * 64], [vok])

        return norm_, first_, second_

    parts = [make_chunk(*c) for c in chunks]
    parts[0][0]()
    for ci_ in range(len(parts)):
        parts[ci_][1]()
        if ci_ + 1 < len(parts):
            parts[ci_ + 1][0]()
        parts[ci_][2]()


Builder.phaseA = _phaseA


def prep_core_inputs(inp, L, NS, core, nl=2, batch=None, mirror=False):
    b, hs = (core // NS) % inp["x"].shape[0], core % NS
    if batch is not None:
        b = batch
    cfg = Cfg(L, NS, hs)
    f = lambda a: np.ascontiguousarray(a, dtype=np.float32)
    d = {}
    d["x"] = f(inp["x"][b, :L]); d["ctx"] = f(inp["ctx"][b])
    cc = np.stack([inp["c"][b], inp["c_ctx"]], 0)
    d["cc"] = f(cc.reshape(2, 8, 128).transpose(2, 0, 1).reshape(128, 16))
    d["ident"] = np.eye(128, dtype=np.float32)
    jj, ii = np.meshgrid(np.arange(128), np.arange(128), indexing="ij")
    d["mnext"] = (ii <= jj).astype(np.float32); d["mprev"] = (jj <= ii).astype(np.float32)
    d["cs_a"], d["sn_a"] = rope_tables(L, 32)
    d["cs_b"], d["sn_b"] = rope_tables(L, 64)
    d["norm_g"] = f(inp["norm_g"][:nl]); d["mod_w"] = f(inp["mod_w"][:nl]); d["mod_b"] = f(inp["mod_b"][:nl])
    d["w_in"] = f(inp["w_in"][:nl][:, :, cfg.cols])
    d["qn"] = f(inp["mla_q_norm"][:nl].reshape(nl, 3, 128).transpose(0, 2, 1))
    d["kvn"] = f(inp["mla_kv_norm"][:nl].reshape(nl, 2, 128).transpose(0, 2, 1))
    sw32 = [(j + 16) % 32 for j in range(32)]
    uq_cols = []
    for h in cfg.HA:
        base = 96 * h
        uq_cols += [base + j for j in range(64)] + [base + 64 + j for j in range(32)] + [base + 64 + sw32[j] for j in range(32)]
    d["w_uq"] = f(inp["mla_w_uq"][:nl][:, :, uq_cols])
    kcols = [128 * h + j for h in cfg.HA for j in range(64)]
    vcols = [128 * h + 64 + j for h in cfg.HA for j in range(64)]
    d["w_ukvk"] = f(inp["mla_w_ukv"][:nl][:, :, kcols]); d["w_ukvv"] = f(inp["mla_w_ukv"][:nl][:, :, vcols])
    d["sink"] = f(inp["swa_sink"][:nl][:, cfg.HB])
    ch = np.array(cfg.CH); ncg = len(ch) // 128
    ucol = np.stack([part * 256 + ch for part in range(3)], 0).reshape(3 * ncg, 128)
    d["hy_cw"] = f(inp["hy_conv_w"][:nl][:, :, ucol].transpose(0, 3, 2, 1))
    d["hy_cb"] = f(inp["hy_conv_b"][:nl][:, ucol].transpose(0, 2, 1))
    d["hy_hbias"] = f(inp["hy_bias"][:nl][:, ch.reshape(ncg, 128)].transpose(0, 2, 1))
    d["hy_w1"] = f(inp["hy_w1"][:nl]); d["hy_w2"] = f(inp["hy_w2"][:nl])
    w3c = np.concatenate([ch, 256 + ch])
    d["hy_w3"] = f(inp["hy_w3"][:nl][:, :, w3c])
    d["hy_b1"] = f(inp["hy_b1"][:nl][:, :, None]); d["hy_b2"] = f(inp["hy_b2"][:nl][:, :, None]); d["hy_fr"] = f(inp["hy_freq"][:nl][:, :, None])
    d["hy_b3"] = f(inp["hy_b3"][:nl][:, w3c.reshape(2 * ncg, 128)].transpose(0, 2, 1))
    fast, slow = math.log(1e-2) / 0.3, math.log(1e-2) / 1.5
    deltas = np.abs(np.linspace(fast, slow, 256, dtype=np.float32))
    d["hy_ndl"] = f(-deltas[ch.reshape(ncg, 128)].T)
    for sn_, Ls_ in (("x", L), ("c", CTX)):
        Tt = fft_tables(Ls_)
        for nm in ("F1", "TT1", "TT2", "BDC", "BDS", "BDSn", "TI1", "TI2", "IC", "ISn"):
            d[f"ft{sn_}_{nm}"] = Tt[nm]
        d["zemb" + sn_], d["tlin" + sn_] = hy_consts(Ls_)
    rows = []
    for r in range(NS):
        rows += Cfg(L, NS, r).wout_rows
    d["w_out"] = f(inp["w_out"][:nl][:, rows, :])
    d["fng"] = f(inp["final_norm_g"])
    d["hy_flag"] = np.full((128, 1), 1.0 if mirror else 0.0, np.float32)
    if mirror:
        d["x"] = f(d["x"][::-1]); d["ctx"] = f(d["ctx"][::-1])
        for k_ in ("cs_a", "sn_a", "cs_b", "sn_b"):
            t_ = d[k_].copy(); t_[:, :L] = t_[:, :L][:, ::-1]; d[k_] = f(t_)
        d["hy_cw"] = f(d["hy_cw"][..., ::-1])
        nch_ = d["hy_w3"].shape[2] // 2
        d["hy_w3"] = f(np.concatenate([d["hy_w3"][:, :, nch_:], d["hy_w3"][:, :, :nch_]], axis=2))
        ncg_ = d["hy_b3"].shape[2] // 2
        d["hy_b3"] = f(np.concatenate([d["hy_b3"][:, :, ncg_:], d["hy_b3"][:, :, :ncg_]], axis=2))
    return d


def _phaseB(self, l, AR, S0, S1, B4, B5):
    cfg, L, LK, P = self.cfg, self.L, self.LK, self.P
    scr = self.scr
    nha = cfg.nha
    update_ctx = (l < self.nlayers - 1)
    NKB = LK // 128
    KT = [AR.alloc([128, LK], BF16) for _ in range(2)]
    VA = [AR.alloc([128, NKB, 128], BF16) for _ in range(2)]
    QTc = [AR.alloc([128, 512], BF16) for _ in range(2)]
    gtc = [AR.alloc([128, 512], BF16) for _ in range(2)]
    pts = [AR.alloc([128, 1024], BF16) for _ in range(3)]
    rden = AR.alloc([128, 512], F32)
    yo = [AR.alloc([128, 512], BF16) for _ in range(2)]
    for i in range(2):
        self.memset(VA[i][:, :, 64:128], 1.0, [f"VA{i}"])
    Stiles = [(S0, "S0"), (S1, "S1")]
    Obanks = [(B4, "B4"), (B5, "B5")]
    cnt = [0, 0, 0]
    for h in range(nha):
        kt, ktk = KT[h % 2], f"KT{h % 2}"
        va, vak = VA[h % 2], f"VA{h % 2}"
        self.ld(kt[0:64, :], scr["KN"][h], writes=[ktk])
        self.ld(kt[64:96, :], scr["KR"], writes=[ktk])
        self.ld(va[:, :, 0:64], scr["VD"][:, h * 64:(h + 1) * 64].rearrange("(kb p) c -> p kb c", p=128), writes=[vak])
        LQ_ = self.LQ if (l == self.nlayers - 1) else L
        qchunks = [(c * 512, 512, 0, NKB) for c in range(LQ_ // 512)]
        if update_ctx:
            qchunks.append((L, CTX, NKB - 2, NKB))
        for (t0, n, kb0, kb1) in qchunks:
            qt, qtk = QTc[cnt[0] % 2], f"QTc{cnt[0] % 2}"
            gt, gtk = gtc[cnt[0] % 2], f"gtc{cnt[0] % 2}"
            O, Ok = Obanks[cnt[0] % 2]
            yy, yk = yo[cnt[0] % 2], f"yo{cnt[0] % 2}"
            cnt[0] += 1
            self.ld(qt[0:96, 0:n], scr["QT"][h, :, t0:t0 + n], writes=[qtk])
            self.ld(gt[0:64, 0:n], scr["GT"][h * 64:(h + 1) * 64, t0:t0 + n], writes=[gtk])
            pairs = list(range(kb0, kb1, 2))

            def emit_s(kb):
                S, Sk = Stiles[cnt[1] % 2]
                cnt[1] += 1
                for u in range(2):
                    self.mm(S[:, u * 512:u * 512 + n], kt[0:96, (kb + u) * 128:(kb + u + 1) * 128], qt[0:96, 0:n],
                            True, True, [ktk, qtk], [Sk])
                pt, ptk = pts[cnt[2] % 3], f"pt{cnt[2] % 3}"
                cnt[2] += 1
                if n == 512:
                    self.act(pt, S, AF.Exp, [Sk], [ptk])
                else:
                    for u in range(2):
                        self.act(pt[:, u * 512:u * 512 + n], S[:, u * 512:u * 512 + n], AF.Exp, [Sk], [ptk])
                return pt, ptk

            def emit_pv(kb, pt, ptk):
                for u in range(2):
                    self.mm(O[:, 0:n], va[:, kb + u, :], pt[:, u * 512:u * 512 + n],
                            (kb == kb0 and u == 0), (kb + u == kb1 - 1), [vak, ptk], [Ok])
            prev = None
            for kb in pairs:
                cur = emit_s(kb)
                if prev is not None:
                    emit_pv(prev[0], prev[1], prev[2])
                prev = (kb, cur[0], cur[1])
            emit_pv(prev[0], prev[1], prev[2])
            self.act(rden[0:64, 0:n], O[64:128, 0:n], AF.Ln, [Ok], ["rden"])
            self.act(rden[0:64, 0:n], rden[0:64, 0:n], AF.Exp, ["rden"], ["rden"], scale=-1.0)
            self.tt(rden[0:64, 0:n], rden[0:64, 0:n], gt[0:64, 0:n], ALU.mult, ["rden", gtk], ["rden"])
            self.tt(yy[0:64, 0:n], O[0:64, 0:n], rden[0:64, 0:n], ALU.mult, [Ok, "rden"], [yk])
            self.stq(scr["YT"][h * 64:(h + 1) * 64, t0:t0 + n], yy[0:64, 0:n], [yk])


def _phaseE(self, l, AR, x_src, c_src, bank, xtile, MODD, wout_in, fng_in, out_ap, YTsrc):
    cfg, L, LK, P = self.cfg, self.L, self.LK, self.P
    scr = self.scr
    last = (l == self.nlayers - 1)
    update_ctx = not last
    Gmod = [AR.alloc([128, D], F32) for _ in range(2)]
    for j in range(2):
        self.ld(Gmod[j], MODD[4 + j], ["MODD"], [f"Gmod{j}"])
    wo = AR.alloc([128, 8, D], BF16)
    yt = [AR.alloc([128, 8, 512], BF16) for _ in range(2)]
    xn = [AR.alloc([128, D], F32) for _ in range(2)]
    tmp = AR.alloc([128, D], F32)
    junk = AR.alloc([128, D], BF16)
    fng = AR.alloc([128, D], F32)
    ss = AR.alloc([128, 2], F32)
    wo_v = wout_in[l].rearrange("(k p) n -> p k n", p=128)
    for k in range(8):
        t_, k_ = xtile()
        self.ld(t_, wo_v[:, k, :], writes=[k_])
        if k % 2:
            self.act(wo[:, k, :], t_, AF.Copy, [k_], ["wo"])
        else:
            self.cp(wo[:, k, :], t_, [k_], ["wo"])
    if last:
        self.ld(fng, fng_in.rearrange("(o d) -> o d", o=1).to_broadcast([128, D]), writes=["fng"])
    LQ_ = self.LQ if last else L
    chunks = [(c * 512, 512, 0) for c in range(LQ_ // 512)]
    if update_ctx:
        chunks.append((L, CTX, 1))
    ci = 0
    for (t0, n, j) in chunks:
        ytt, ytk = yt[ci % 2], f"yt{ci % 2}"
        ci += 1
        self.ld(ytt[:, :, 0:n], YTsrc.rearrange("(k p) t -> p k t", p=128)[:, :, t0:t0 + n], writes=[ytk])
        src = c_src if j else x_src
        dst = (scr["XC1"] if j else scr["X1"])
        s0 = 0 if j else t0
        for i in range(n // 128):
            xt_, xk = xtile()
            self.ld(xt_, src[s0 + i * 128:s0 + (i + 1) * 128, :], writes=[xk])
            xx, xxk = xn[i % 2], f"xn{i % 2}"
            for hf in range(2):
                ps, pk = bank()
                for k in range(8):
                    self.mm(ps, ytt[:, k, i * 128:(i + 1) * 128], wo[:, k, hf * 512:(hf + 1) * 512], k == 0, k == 7, [ytk, "wo"], [pk])
                cs_ = slice(hf * 512, hf * 512 + 512)
                self.tt(tmp[:, cs_], ps, Gmod[j][:, cs_], ALU.mult, [pk, f"Gmod{j}"], ["tmpE"])
                self.tt(xx[:, cs_], tmp[:, cs_], xt_[:, cs_], ALU.add, ["tmpE", xk], [xxk])
            if not last:
                self.stq(dst[s0 + i * 128:s0 + (i + 1) * 128, :], xx, [xxk])
            else:
                self.act(junk, xx, AF.Square, [xxk], ["junkE", "ssE"], accum=ss[:, 0:1])
                self.ts(ss[:, 1:2], ss[:, 0:1], 1.0 / D, ALU.mult, ["ssE"], ["ssE2"], s2=EPS, op1=ALU.add)
                self.act(ss[:, 1:2], ss[:, 1:2], AF.Sqrt, ["ssE2"], ["ssE2"])
                self.recip(ss[:, 1:2], ss[:, 1:2], ["ssE2"], ["ssE2"])
                self.stt(tmp, xx, ss[:, 1:2], fng, ALU.mult, ALU.mult, [xxk, "ssE2", "fng"], ["tmpE"])
                self.stq(out_ap[s0 + i * 128:s0 + (i + 1) * 128, :], tmp, ["tmpE"])


Builder.phaseB = _phaseB
Builder.phaseE = _phaseE


def _phaseC(self, l, AR, S0, S1, B4, B5, sink_in, mnext_in, mprev_in, xtile):
    cfg, L, LK, P = self.cfg, self.L, self.LK, self.P
    scr = self.scr
    nha, nhb, ngb = cfg.nha, cfg.nhb, cfg.ngb
    update_ctx = (l < self.nlayers - 1)
    NKB = LK // 128
    NLB = L // 128
    skt = [AR.alloc([128, LK], BF16) for _ in range(2)]
    sva = [AR.alloc([128, NKB, 128], BF16) for _ in range(2)]
    sqc = [AR.alloc([128, 512], BF16) for _ in range(2)]
    gtc = [AR.alloc([128, 512], BF16) for _ in range(2)]
    pts = [AR.alloc([128, 512], BF16) for _ in range(4)]
    rden = AR.alloc([128, 512], F32)
    yo = [AR.alloc([128, 512], BF16) for _ in range(2)]
    mnext = AR.alloc([128, 128], BF16); mprev = AR.alloc([128, 128], BF16)
    es = AR.alloc([128, nhb], F32)
    t_, k_ = xtile()
    self.ld(t_[:, 0:128], mnext_in, writes=[k_]); self.cp(mnext, t_[:, 0:128], [k_], ["mnext"])
    t_, k_ = xtile()
    self.ld(t_[:, 0:128], mprev_in, writes=[k_]); self.cp(mprev, t_[:, 0:128], [k_], ["mprev"])
    self.ld(es, sink_in[l:l + 1, :].to_broadcast([128, nhb]), writes=["es"])
    self.act(es, es, AF.Exp, ["es"], ["es"])
    for i in range(2):
        self.memset(sva[i][:, :, 64:128], 1.0, [f"sva{i}"])
    Sb = [(S0[:, 0:512], "S0a"), (S0[:, 512:1024], "S0b"), (S1[:, 0:512], "S1a"), (S1[:, 512:1024], "S1b")]
    Obanks = [(B4, "B4"), (B5, "B5")]
    cnt = [0, 0]
    for gi in range(ngb):
        kt, ktk = skt[gi % 2], f"skt{gi % 2}"
        va, vak = sva[gi % 2], f"sva{gi % 2}"
        self.ld(kt[0:64, :], scr["SKT"][gi], writes=[ktk])
        self.ld(va[:, :, 0:64], scr["SVD"][:, gi * 64:(gi + 1) * 64].rearrange("(kb p) c -> p kb c", p=128), writes=[vak])
        for r in range(3):
            hi = 3 * gi + r
            LQ_ = self.LQ if (l == self.nlayers - 1) else L
            qchunks = [(c * 512, 512, False) for c in range(LQ_ // 512)]
            if update_ctx:
                qchunks.append((L, CTX, True))
            for (t0, n, isctx) in qchunks:
                sq, sqk = sqc[cnt[0] % 2], f"sqc{cnt[0] % 2}"
                gt, gtk = gtc[cnt[0] % 2], f"gtcC{cnt[0] % 2}"
                O, Ok = Obanks[cnt[0] % 2]
                yy, yk = yo[cnt[0] % 2], f"yoC{cnt[0] % 2}"
                cnt[0] += 1
                self.ld(sq[0:64, 0:n], scr["SQT"][hi, :, t0:t0 + n], writes=[sqk])
                grow = 64 * nha + 64 * hi
                self.ld(gt[0:64, 0:n], scr["GT"][grow:grow + 64, t0:t0 + n], writes=[gtk])
                items = [(NKB - 2, 0, n, []), (NKB - 1, 0, n, [])]
                if not isctx:
                    qb0 = t0 // 128
                    for kb in range(max(0, qb0 - 1), min(NLB - 1, qb0 + 4) + 1):
                        lo, hi_ = max(kb - 1, qb0), min(kb + 1, qb0 + 3)
                        masks = []
                        for qb in range(lo, hi_ + 1):
                            if qb == kb + 1:
                                masks.append(((qb - lo) * 128, "next"))
                            elif qb == kb - 1:
                                masks.append(((qb - lo) * 128, "prev"))
                        items.append((kb, (lo - qb0) * 128, (hi_ - lo + 1) * 128, masks))
                def emit_s(it):
                    kb, c0, ncol, masks = it
                    S, Sk = Sb[cnt[1] % 4]
                    pt, ptk = pts[cnt[1] % 4], f"ptC{cnt[1] % 4}"
                    cnt[1] += 1
                    self.mm(S[:, 0:ncol], kt[0:64, kb * 128:(kb + 1) * 128], sq[0:64, c0:c0 + ncol], True, True, [ktk, sqk], [Sk])
                    self.act(pt[:, 0:ncol], S[:, 0:ncol], AF.Exp, [Sk], [ptk])
                    for (sc0, mt) in masks:
                        m_, mk = (mnext, "mnext") if mt == "next" else (mprev, "mprev")
                        self.tt(pt[:, sc0:sc0 + 128], pt[:, sc0:sc0 + 128], m_, ALU.mult, [ptk, mk], [ptk])
                    return pt, ptk

                def emit_pv(ii, it, pt, ptk):
                    kb, c0, ncol, masks = it
                    self.mm(O[:, c0:c0 + ncol], va[:, kb, :], pt[:, 0:ncol], ii == 0, ii == len(items) - 1, [vak, ptk], [Ok])
                LOOK = 2
                pend = []
                for ii, it in enumerate(items):
                    pend.append((ii, it) + emit_s(it))
                    if len(pend) > LOOK:
                        emit_pv(*pend.pop(0))
                while pend:
                    emit_pv(*pend.pop(0))
                self.ts(rden[0:64, 0:n], O[64:128, 0:n], es[64:128, hi:hi + 1], ALU.add, [Ok, "es"], ["rdenC"])
                self.act(rden[0:64, 0:n], rden[0:64, 0:n], AF.Ln, ["rdenC"], ["rdenC"])
                self.act(rden[0:64, 0:n], rden[0:64, 0:n], AF.Exp, ["rdenC"], ["rdenC"], scale=-1.0)
                self.tt(rden[0:64, 0:n], rden[0:64, 0:n], gt[0:64, 0:n], ALU.mult, ["rdenC", gtk], ["rdenC"])
                self.tt(yy[0:64, 0:n], O[0:64, 0:n], rden[0:64, 0:n], ALU.mult, [Ok, "rdenC"], [yk])
                self.stq(scr["YT"][grow:grow + 64, t0:t0 + n], yy[0:64, 0:n], [yk])


Builder.phaseC = _phaseC


def fft_tables(Ls):
    N = 2 * Ls
    N2 = N // 64
    KA = min(128, N2)
    NA = N2 // KA
    n2 = np.arange(N2)[:, None].astype(np.float64); ka = np.arange(N2)[None, :].astype(np.float64)
    a1 = 2 * np.pi * n2 * ka / N2
    F1 = np.concatenate([np.cos(a1), -np.sin(a1)], axis=1)
    F1 = F1[0:N2 // 2]
    NB = PB = 0
    n1 = np.arange(64)[:, None].astype(np.float64)
    at = 2 * np.pi * n1 * ka / N
    TC, TS = np.cos(at), np.sin(at)
    TC2, TS2 = np.concatenate([TC, TC], 0), np.concatenate([TS, TS], 0)
    TT1 = np.concatenate([TC2, TS2], 1); TT2 = np.concatenate([-TS2, TC2], 1)
    kb = np.arange(64)[None, :].astype(np.float64)
    a64 = 2 * np.pi * n1 * kb / 64
    C64, S64 = np.cos(a64), np.sin(a64)
    Z = np.zeros((64, 64))
    BDC = np.block([[C64, Z], [Z, C64]]); BDS = np.block([[S64, Z], [Z, S64]])
    TCT = np.concatenate([TC.T, TC.T], 1).reshape(NA, KA, 128).transpose(1, 0, 2)
    TST = np.concatenate([TS.T, TS.T], 1).reshape(NA, KA, 128).transpose(1, 0, 2)
    TI1 = np.concatenate([TCT, -TST], 2); TI2 = np.concatenate([TST, TCT], 2)
    K1 = N2 // 2
    ai = 2 * np.pi * np.arange(N2)[:, None].astype(np.float64) * np.arange(K1)[None, :] / N2
    IC = np.cos(ai).reshape(NA, KA, K1).transpose(1, 0, 2); ISn = (-np.sin(ai)).reshape(NA, KA, K1).transpose(1, 0, 2)
    f = lambda a: np.ascontiguousarray(a, np.float32)
    return dict(F1=f(F1), TT1=f(TT1), TT2=f(TT2), BDC=f(BDC), BDS=f(BDS), BDSn=f(-BDS), TI1=f(TI1), TI2=f(TI2),
                IC=f(IC), ISn=f(ISn), N2=N2, KA=KA, NA=NA, NB=NB, PB=PB, K1=K1)


def hy_consts(Ls):
    t = np.linspace(0.0, 1.0, Ls, dtype=np.float32)[:, None]
    w = (np.float32(2.0 * math.pi / Ls) * np.arange(Ls, dtype=np.float32))[:, None]
    bands = np.linspace(1e-4, 7, 8, dtype=np.float32)[None, :]
    z = np.concatenate([t, np.cos(bands * w), -np.sin(bands * w)], axis=-1).astype(np.float32)
    return np.ascontiguousarray(z.T), np.ascontiguousarray(t.T)


def _fft_forward(self, T, srcL, srck, nblk, p, X, Xk, tw, twk, rotg):
    N2 = T["N2"]
    G, Gk = rotg()
    PBk = T["PBk"]
    self.mm(G[:, 0:2 * N2], srcL[0:PBk, 0, 2 * p:2 * p + 2, :], T["F1"][0:PBk, 0, :], True, True, [srck, "F1"], [Gk])
    A, Ak = tw[0], twk + "a"
    Bt, Bk = tw[1], twk + "b"
    Gp, Gpk = tw[2], twk + "g"
    self.tt(A[:, 0:2 * N2], G[:, 0:2 * N2], T["TT1"], ALU.mult, [Gk, "TT"], [Ak])
    self.tt(Bt[:, 0:2 * N2], G[:, 0:2 * N2], T["TT2"], ALU.mult, [Gk, "TT"], [Bk])
    self.tt(Gp[:, 0:N2], A[:, 0:N2], A[:, N2:2 * N2], ALU.add, [Ak], [Gpk])
    self.tt(Gp[:, N2:2 * N2], Bt[:, 0:N2], Bt[:, N2:2 * N2], ALU.add, [Bk], [Gpk])
    self.mm(X[:, 0:N2], T["BDC"], Gp[:, 0:N2], True, False, ["BD", Gpk], [Xk])
    self.mm(X[:, 0:N2], T["BDS"], Gp[:, N2:2 * N2], False, True, ["BD", Gpk], [Xk])
    self.mm(X[:, N2:2 * N2], T["BDC"], Gp[:, N2:2 * N2], True, False, ["BD", Gpk], [Xk])
    self.mm(X[:, N2:2 * N2], T["BDSn"], Gp[:, 0:N2], False, True, ["BD", Gpk], [Xk])


Builder.fft_forward = _fft_forward


def _hy_common(self, l):
    cfg = self.cfg
    update_ctx = (l < self.nlayers - 1)
    seqs = [("x", self.L, 0)]
    if update_ctx:
        seqs.append(("c", CTX, self.L))
    return seqs


def _hy_setup(self, l, AR, HW, xtile):
    cfg, P = self.cfg, self.P
    nch = cfg.nch
    ncg = nch // 128
    H = dict(l=l)
    H["cw"] = AR.alloc([128, 3 * ncg, 3], F32); H["cb"] = AR.alloc([128, 3 * ncg], F32); H["hbias"] = AR.alloc([128, ncg], F32)
    H["w1"] = AR.alloc([128, 64], F32); H["w2"] = AR.alloc([128, 64], F32); H["w3"] = AR.alloc([128, 2 * nch], F32)
    H["b1"] = AR.alloc([128, 1], F32); H["b2"] = AR.alloc([128, 1], F32); H["fr"] = AR.alloc([128, 1], F32)
    H["b3"] = AR.alloc([128, 2 * ncg], F32); H["ndl"] = AR.alloc([128, ncg], F32)
    self.ld(H["cw"], HW["cw"][l], writes=["cw"]); self.ld(H["cb"], HW["cb"][l], writes=["cb"]); self.ld(H["hbias"], HW["hbias"][l], writes=["hbias"])
    self.ld(H["w1"][0:17, :], HW["w1"][l], writes=["w1"]); self.ld(H["w2"][0:64, :], HW["w2"][l], writes=["w2"])
    self.ld(H["w3"][0:64, :], HW["w3"][l], writes=["w3"])
    self.ld(H["b1"][0:64, :], HW["b1"][l], writes=["b1"]); self.ld(H["b2"][0:64, :], HW["b2"][l], writes=["b2"])
    self.ld(H["fr"][0:64, :], HW["fr"][l], writes=["fr"]); self.ld(H["b3"], HW["b3"][l], writes=["b3"]); self.ld(H["ndl"], HW["ndl"], writes=["ndl"])
    for (sname, Ls, toff) in self.hy_common(l):
        Tn = HW["T" + sname]
        N2, KA, NA, K1 = Tn["N2"], Tn["KA"], Tn["NA"], Tn["K1"]
        T = dict(N2=N2, KA=KA, NA=NA, K1=K1, PBk=K1, sn=sname)
        T["F1"] = AR.alloc([128, 1, 2 * N2], BF16)
        T["BDC"] = AR.alloc([128, 128], BF16); T["BDS"] = AR.alloc([128, 128], BF16); T["BDSn"] = AR.alloc([128, 128], BF16)
        T["IC"] = AR.alloc([128, NA, K1], BF16); T["ISn"] = AR.alloc([128, NA, K1], BF16)
        T["TT1"] = AR.alloc([128, 2 * N2], F32); T["TT2"] = AR.alloc([128, 2 * N2], F32)
        T["TI1"] = AR.alloc([128, NA * 256], F32); T["TI2"] = AR.alloc([128, NA * 256], F32)
        t_, k_ = xtile()
        self.ld(t_[0:K1, 0:2 * N2], Tn["F1"], writes=[k_]); self.cp(T["F1"][0:K1, 0, :], t_[0:K1, 0:2 * N2], [k_], ["F1" + sname])
        for nm in ("BDC", "BDS", "BDSn"):
            t_, k_ = xtile()
            self.ld(t_[:, 0:128], Tn[nm], writes=[k_]); self.cp(T[nm], t_[:, 0:128], [k_], ["BD" + sname])
        for nm in ("IC", "ISn"):
            t_, k_ = xtile()
            tv = t_[0:KA, 0:NA * K1].rearrange("p (a n) -> p a n", a=NA)
            self.ld(tv, Tn[nm], writes=[k_]); self.cp(T[nm][0:KA], tv, [k_], ["ICS" + sname])
        self.ld(T["TT1"], Tn["TT1"], writes=["TT" + sname]); self.ld(T["TT2"], Tn["TT2"], writes=["TT" + sname])
        self.ld(T["TI1"][0:KA], Tn["TI1"].rearrange("p a n -> p (a n)"), writes=["TI" + sname])
        self.ld(T["TI2"][0:KA], Tn["TI2"].rearrange("p a n -> p (a n)"), writes=["TI" + sname])
        H["T" + sname] = T
    H["flag"] = AR.alloc([128, 1], F32)
    self.ld(H["flag"], HW["flag"], writes=["flag"])
    H["mark"] = AR.mark()
    return H


def _hy_pre(self, H, AR, S0, S1, HW):
    cfg, L, LK, P = self.cfg, self.L, self.LK, self.P
    scr = self.scr
    l = H["l"]
    nch = cfg.nch
    ncg = nch // 128
    cw, cb, hbias, w1, w2, w3, b1, b2, fr, b3, ndl = (H[k] for k in ("cw", "cb", "hbias", "w1", "w2", "w3", "b1", "b2", "fr", "b3", "ndl"))
    flag = H["flag"]
    Gb = [(S0[:, 0:512], "S0a"), (S0[:, 512:1024], "S0b"), (S1[:, 0:512], "S1a"), (S1[:, 512:1024], "S1b")]
    gst = [0]

    def rotg():
        b = Gb[gst[0] % 4]
        gst[0] += 1
        return b
    TWO_PI = float(2 * np.pi)
    mark2 = H["mark"]
    for (sname, Ls, toff) in self.hy_common(l):
        N = 2 * Ls
        P.barrier()
        AR.reset(mark2)
        zemb_in, tlin_in = HW["zemb" + sname], HW["tlin" + sname]
        h2all = AR.alloc([128, Ls], F32)
        hk = [AR.alloc([128, Ls], F32) for _ in range(2)]
        zcL = [AR.alloc([128, 512], F32) for _ in range(2)]; argL = [AR.alloc([128, 512], F32) for _ in range(2)]
        kfL = [AR.alloc([128, 512], F32) for _ in range(2)]
        kiL = [AR.alloc([128, 512], I32) for _ in range(2)]; h1L = [AR.alloc([128, 512], F32) for _ in range(2)]
        tlL = [AR.alloc([128, 512], F32) for _ in range(2)]; decL = [AR.alloc([128, 512], F32) for _ in range(2)]
        asum = AR.alloc([128, 4], F32)
        hn = [AR.alloc([128, 512], BF16) for _ in range(2)]
        junkf = AR.alloc([128, 512], BF16)
        asumc = AR.alloc([128, 2 * ((Ls + 511) // 512)], F32)

        def sin_layer(ps, pk, n, bias, outap, outk, bi):
            arg, kf_, ki = argL[bi], kfL[bi], kiL[bi]
            ak, kk, ik = f"arg{bi}", f"kf{bi}", f"ki{bi}"
            self.ts(arg[0:64, 0:n], ps[0:64, 0:n], bias[0:64, 0:1], ALU.add, [pk, "b12", "fr"], [ak], s2=fr[0:64, 0:1], op1=ALU.mult)
            self.ts(ki[0:64, 0:n], arg[0:64, 0:n], 1.0 / TWO_PI, ALU.mult, [ak], [ik])
            self.cp(kf_[0:64, 0:n], ki[0:64, 0:n], [ik], [kk])
            self.stt(arg[0:64, 0:n], kf_[0:64, 0:n], -TWO_PI, arg[0:64, 0:n], ALU.mult, ALU.add, [kk, ak], [ak])
            self.ts(kf_[0:64, 0:n], arg[0:64, 0:n], float(np.pi), ALU.is_gt, [ak], [kk], s2=-TWO_PI, op1=ALU.mult)
            self.tt(arg[0:64, 0:n], arg[0:64, 0:n], kf_[0:64, 0:n], ALU.add, [ak, kk], [ak])
            self.ts(kf_[0:64, 0:n], arg[0:64, 0:n], float(-np.pi), ALU.is_lt, [ak], [kk], s2=TWO_PI, op1=ALU.mult)
            self.tt(arg[0:64, 0:n], arg[0:64, 0:n], kf_[0:64, 0:n], ALU.add, [ak, kk], [ak])
            self.act(outap, arg[0:64, 0:n], AF.Sin, [ak], [outk])

        chs = [(c0, min(512, Ls - c0)) for c0 in range(0, Ls, 512)]
        for g0 in range(0, len(chs), 2):
            grp = list(enumerate(chs[g0:g0 + 2]))
            pss = {}
            for bi, (c0, n) in grp:
                self.ld(zcL[bi][0:17, 0:n], zemb_in[:, c0:c0 + n], writes=[f"zc{bi}"])
                pss[bi] = rotg()
                self.mm(pss[bi][0][0:64, 0:n], w1[0:17, :], zcL[bi][0:17, 0:n], True, True, ["w1", f"zc{bi}"], [pss[bi][1]])
            for bi, (c0, n) in grp:
                sin_layer(pss[bi][0], pss[bi][1], n, b1, h1L[bi][0:64, 0:n], f"h1{bi}", bi)
            for bi, (c0, n) in grp:
                pss[bi] = rotg()
                self.mm(pss[bi][0][0:64, 0:n], w2[0:64, :], h1L[bi][0:64, 0:n], True, True, ["w2", f"h1{bi}"], [pss[bi][1]])
            for bi, (c0, n) in grp:
                sin_layer(pss[bi][0], pss[bi][1], n, b2, h2all[0:64, c0:c0 + n], "h2all", bi)
        for cg in range(ncg):
            for ic, (c0, n) in enumerate(chs):
                bi = ic % 2
                tl, dec = tlL[bi], decL[bi]
                self.ld(tl[:, 0:n], tlin_in[:, c0:c0 + n].to_broadcast([128, n]), writes=[f"tl{bi}"])
                self.act(dec[:, 0:n], tl[:, 0:n], AF.Exp, [f"tl{bi}", "ndl"], [f"dec{bi}"], scale=ndl[:, cg:cg + 1])
                for d_ in range(2):
                    ps, pk = rotg()
                    col = d_ * nch + cg * 128
                    self.mm(ps[:, 0:n], w3[0:64, col:col + 128], h2all[0:64, c0:c0 + n], True, True, ["w3", "h2all"], [pk])
                    self.stt(hk[d_][:, c0:c0 + n], ps[:, 0:n], b3[:, d_ * ncg + cg:d_ * ncg + cg + 1], dec[:, 0:n], ALU.add, ALU.mult,
                             [pk, "b3", f"dec{bi}"], [f"hk{d_}"])
            self.tt(asum[:, 2:3], hk[1][:, 0:1], hk[0][:, 0:1], ALU.subtract, ["hk0", "hk1"], ["asum2"])
            self.stt(hk[0][:, 0:1], asum[:, 2:3], flag[:, 0:1], hk[0][:, 0:1], ALU.mult, ALU.add, ["asum2", "flag", "hk0"], ["hk0"])
            for d_ in range(2):
                if d_ == 1:
                    self.memset(hk[1][:, 0:1], 0.0, ["hk1"])
                ncks = len(chs)
                for ic, (c0, n) in enumerate(chs):
                    self.act(junkf[:, 0:n], hk[d_][:, c0:c0 + n], AF.Abs, [f"hk{d_}"], ["junkf", "asumc"],
                             accum=asumc[:, d_ * ncks + ic:d_ * ncks + ic + 1])
                self.act(junkf[:, 0:ncks], asumc[:, d_ * ncks:(d_ + 1) * ncks], AF.Copy, ["asumc"], ["junkf", "asum"],
                         accum=asum[:, d_:d_ + 1])
            self.tt(asum[:, 2:3], asum[:, 0:1], asum[:, 1:2], ALU.add, ["asum"], ["asum2"])
            self.ts(asum[:, 2:3], asum[:, 2:3], EPS, ALU.add, ["asum2"], ["asum2"], s2=float(N), op1=ALU.mult)
            self.recip(asum[:, 3:4], asum[:, 2:3], ["asum2"], ["asum3"])
            hi_ = 0
            for d_ in range(2):
                for (c0, n) in chs:
                    ht, htk = hn[hi_ % 2], f"hn{hi_ % 2}"
                    hi_ += 1
                    self.ts(ht[:, 0:n], hk[d_][:, c0:c0 + n], asum[:, 3:4], ALU.mult, [f"hk{d_}", "asum3"], [htk])
                    self.stq(scr["HND"][d_, cg * 128:(cg + 1) * 128, toff + c0:toff + c0 + n], ht[:, 0:n], [htk], ["HND"])

        def load_u(ubt, ubk, parts, cg, c0, n):
            for pi_, part in enumerate(parts):
                row = part * nch + cg * 128
                lo, hi_ = max(c0 - 1, 0), min(c0 + n + 1, Ls)
                if c0 == 0:
                    self.memset(ubt[:, pi_, 0:1], 0.0, [ubk])
                if c0 + n == Ls:
                    self.memset(ubt[:, pi_, n + 1:n + 2], 0.0, [ubk])
                self.ld(ubt[:, pi_, lo - (c0 - 1):hi_ - (c0 - 1)], scr["UT"][row:row + 128, toff + lo:toff + hi_], writes=[ubk])

        def conv(ubt, ubk, pi_, part, cg, n, out, outk):
            j = part * ncg + cg
            self.ts(out[:, 0:n], ubt[:, pi_, 0:n], cw[:, j, 0:1], ALU.mult, [ubk, "cw", "cb"], [outk], s2=cb[:, j:j + 1], op1=ALU.add)
            self.stt(out[:, 0:n], ubt[:, pi_, 1:n + 1], cw[:, j, 1:2], out[:, 0:n], ALU.mult, ALU.add, [ubk, "cw", outk], [outk])
            self.stt(out[:, 0:n], ubt[:, pi_, 2:n + 2], cw[:, j, 2:3], out[:, 0:n], ALU.mult, ALU.add, [ubk, "cw", outk], [outk])


def _hy_post(self, H, AR):
    cfg, L, LK, P = self.cfg, self.L, self.LK, self.P
    scr = self.scr
    l = H["l"]
    nch = cfg.nch
    ncg = nch // 128
    cw, cb, hbias = H["cw"], H["cb"], H["hbias"]
    mark2 = H["mark"]
    for (sname, Ls, toff) in self.hy_common(l):
        Lo = self.LQ if (sname == "x" and l == self.nlayers - 1) else Ls
        chs = [(c0, min(512, Ls - c0)) for c0 in range(0, Lo, 512)]
        def load_u(ubt, ubk, parts, cg, c0, n):
            for pi_, part in enumerate(parts):
                row = part * nch + cg * 128
                lo, hi_ = max(c0 - 1, 0), min(c0 + n + 1, Ls)
                if c0 == 0:
                    self.memset(ubt[:, pi_, 0:1], 0.0, [ubk])
                if c0 + n == Ls:
                    self.memset(ubt[:, pi_, n + 1:n + 2], 0.0, [ubk])
                self.ld(ubt[:, pi_, lo - (c0 - 1):hi_ - (c0 - 1)], scr["UT"][row:row + 128, toff + lo:toff + hi_], writes=[ubk])

        def conv(ubt, ubk, pi_, part, cg, n, out, outk):
            j = part * ncg + cg
            self.ts(out[:, 0:n], ubt[:, pi_, 0:n], cw[:, j, 0:1], ALU.mult, [ubk, "cw", "cb"], [outk], s2=cb[:, j:j + 1], op1=ALU.add)
            self.stt(out[:, 0:n], ubt[:, pi_, 1:n + 1], cw[:, j, 1:2], out[:, 0:n], ALU.mult, ALU.add, [ubk, "cw", outk], [outk])
            self.stt(out[:, 0:n], ubt[:, pi_, 2:n + 2], cw[:, j, 2:3], out[:, 0:n], ALU.mult, ALU.add, [ubk, "cw", outk], [outk])

        P.barrier()
        AR.reset(mark2)
        ub = [AR.alloc([128, 3, 514], BF16) for _ in range(2)]
        acc = [AR.alloc([128, 512], F32) for _ in range(2)]
        zi = [AR.alloc([128, 512], BF16) for _ in range(2)]
        yci = [AR.alloc([128, 512], F32) for _ in range(2)]
        gi_ = [AR.alloc([128, 512], BF16) for _ in range(2)]
        yo = [AR.alloc([128, 512], BF16) for _ in range(2)]
        ci = 0
        grow0 = 64 * cfg.nha + 64 * cfg.nhb
        for cg in range(ncg):
            for (c0, n) in chs:
                s_ = ci % 2
                ci += 1
                ubt, ubk = ub[s_], f"ub{s_}"
                load_u(ubt, ubk, [0], cg, c0, n)
                self.ld(zi[s_][:, 0:n], scr["ZD"][cg * 128:(cg + 1) * 128, toff + c0:toff + c0 + n], ["ZD"], [f"zi{s_}"])
                self.ld(yci[s_][:, 0:n], scr["YC"][cg * 128:(cg + 1) * 128, toff + c0:toff + c0 + n], ["YC"], [f"yci{s_}"])
                gr = grow0 + cg * 128
                self.ld(gi_[s_][:, 0:n], scr["GT"][gr:gr + 128, toff + c0:toff + c0 + n], writes=[f"gi{s_}"])
                a_, ak = acc[s_], f"accD{s_}"
                conv(ubt, ubk, 0, 0, cg, n, a_, ak)
                y1, y1k = yci[s_], f"yci{s_}"
                self.stt(y1[:, 0:n], zi[s_][:, 0:n], hbias[:, cg:cg + 1], y1[:, 0:n], ALU.mult, ALU.add, [f"zi{s_}", "hbias", y1k], [y1k])
                self.tt(y1[:, 0:n], y1[:, 0:n], a_[:, 0:n], ALU.mult, [y1k, ak], [y1k])
                self.tt(yo[s_][:, 0:n], y1[:, 0:n], gi_[s_][:, 0:n], ALU.mult, [y1k, f"gi{s_}"], [f"yoD{s_}"])
                self.stq(scr["YT"][gr:gr + 128, toff + c0:toff + c0 + n], yo[s_][:, 0:n], [f"yoD{s_}"])


Builder.hy_common = _hy_common
Builder.hy_setup = _hy_setup
Builder.hy_pre = _hy_pre
Builder.hy_post = _hy_post


SEQ = 8192
NSPLIT = 1
HALF_LAST = True
_CACHE = {}


def kernel(**inputs):
    inp = {k: np.asarray(v) for k, v in inputs.items()}
    L = inp["x"].shape[1]
    nb = inp["x"].shape[0]
    key = (L, NSPLIT)
    if key not in _CACHE:
        B = Builder(L, NSPLIT, nlayers=2, debug=False, phases="ABCDE", half_last=HALF_LAST)
        nc = B.build()
        _CACHE[key] = (B, nc)
    B, nc = _CACHE[key]
    maps = []
    for core in range(8):
        if HALF_LAST:
            d = prep_core_inputs(inp, L, NSPLIT, 0, nl=2, batch=(core // 2) % nb, mirror=bool(core % 2))
        else:
            d = prep_core_inputs(inp, L, NSPLIT, core, nl=2)
        maps.append({k: d[k] for k in B.inputs})
    res = run_bass_kernel_spmd(nc, maps, core_ids=list(range(8)))
    out = np.empty((nb, L, D), np.float32)
    for b in range(nb):
        if HALF_LAST:
            out[b, :L // 2] = np.asarray(res.results[2 * b]["out"], dtype=np.float32)
            out[b, L // 2:] = np.asarray(res.results[2 * b + 1]["out"], dtype=np.float32)[::-1]
        else:
            out[b] = np.asarray(res.results[b * NSPLIT]["out"], dtype=np.float32)
    return out


def _hy_fft_gen(self, H, AR, Gb_, Xb_, HTb_):
    cfg, L, LK, P = self.cfg, self.L, self.LK, self.P
    scr = self.scr
    l = H["l"]
    nch = cfg.nch
    npairs = nch // 2
    seqs = self.hy_common(l)
    N2m = H["Tx"]["N2"]
    NAm = H["Tx"]["NA"]
    G, Gk = Gb_
    X, Xk = Xb_
    HT, HTk = HTb_
    twA = AR.alloc([128, 2 * N2m], F32); twB = AR.alloc([128, 2 * N2m], F32)
    Gp = [AR.alloc([128, 2 * N2m], BF16) for _ in range(2)]
    kfo = [AR.alloc([128, 2 * N2m], F32) for _ in range(2)]
    kfi = [AR.alloc([128, 2 * N2m], F32) for _ in range(3)]
    ya = AR.alloc([128, 2 * N2m], F32); yb_ = AR.alloc([128, 2 * N2m], F32)
    Yt = [AR.alloc([128, 2 * N2m], BF16) for _ in range(2)]
    ha = AR.alloc([128, NAm * 256], F32); hb_ = AR.alloc([128, NAm * 256], F32)
    Hp = [AR.alloc([128, 2, NAm, 512], BF16) for _ in range(2)]
    yst = [AR.alloc([128, 512], F32) for _ in range(2)]
    CB = 32
    blk = [[AR.alloc([128, CB, 64], BF16) for _ in range(2)] for _ in range(2)]
    d2tiles = ([AR.alloc([128, 2, 514], BF16) for _ in range(2)], [AR.alloc([128, 512], F32) for _ in range(2)],
               [AR.alloc([128, 512], BF16) for _ in range(2)])
    yield
    for (sname, Ls, toff) in seqs:
        T = H["T" + sname]
        N2, KA, NA, K1 = T["N2"], T["KA"], T["NA"], T["K1"]
        sn = sname
        jobs = []
        tails = {}

        gZ = self.hy_d2_gen(H, d2tiles, sname, Ls, toff)
        zstate = [True]

        def adv_z(n_):
            for _ in range(n_):
                if zstate[0]:
                    try:
                        next(gZ)
                    except StopIteration:
                        zstate[0] = False

        def load_blk(src_ap, d_, b, key):
            t_ = blk[d_][b % 2]
            self.ld(t_[0:K1, :, :], src_ap[b * CB:(b + 1) * CB, :].rearrange("c (n2 n1) -> n2 c n1", n1=64),
                    [key], [f"blk{d_}{b % 2}"])

        def st_s1(src_t, srck, pl):
            def f():
                self.mm(G[:, 0:2 * N2], src_t[0:K1, 2 * pl:2 * pl + 2, :], T["F1"][0:K1, 0, :], True, True, [srck, "F1" + sn], [Gk])
            return f

        def st_tw(slot):
            def f():
                gp = Gp[slot]
                self.tt(twA[:, 0:2 * N2], G[:, 0:2 * N2], T["TT1"], ALU.mult, [Gk, "TT" + sn], ["twA"])
                self.tt(twB[:, 0:2 * N2], G[:, 0:2 * N2], T["TT2"], ALU.mult, [Gk, "TT" + sn], ["twB"])
                self.tt(gp[:, 0:N2], twA[:, 0:N2], twA[:, N2:2 * N2], ALU.add, ["twA"], [f"Gp{slot}"])
                self.tt(gp[:, N2:2 * N2], twB[:, 0:N2], twB[:, N2:2 * N2], ALU.add, ["twB"], [f"Gp{slot}"])
            return f

        def st_s3(slot):
            def f():
                gp, gk = Gp[slot], f"Gp{slot}"
                self.mm(X[:, 0:N2], T["BDC"], gp[:, 0:N2], True, False, ["BD" + sn, gk], [Xk])
                self.mm(X[:, 0:N2], T["BDS"], gp[:, N2:2 * N2], False, True, ["BD" + sn, gk], [Xk])
                self.mm(X[:, N2:2 * N2], T["BDC"], gp[:, N2:2 * N2], True, False, ["BD" + sn, gk], [Xk])
                self.mm(X[:, N2:2 * N2], T["BDSn"], gp[:, 0:N2], False, True, ["BD" + sn, gk], [Xk])
            return f

        nblk = nch // CB
        ppb = CB // 2
        jn = 0
        for p in range(npairs):
            b, pl = p // ppb, p % ppb
            for d_ in range(2):
                stages = []
                pre = []
                if pl == 0 and d_ == 0:
                    for dd in range(2):
                        pre.append((lambda dd=dd, b=b: load_blk(scr["HND"][dd, :, toff:toff + Ls], dd, b, "HND")))
                slot = jn % 2
                s1 = st_s1(blk[d_][b % 2], f"blk{d_}{b % 2}", pl)
                stages.append((lambda pre=pre, s1=s1: ([f() for f in pre], s1())))
                stages.append(st_tw(slot))
                stages.append(st_s3(slot))

                def comb(p=p, d_=d_):
                    ko, kok = kfo[p % 2], f"kfo{p % 2}"
                    if d_ == 0:
                        self.cp(ko[:, 0:2 * N2], X[:, 0:2 * N2], [Xk], [kok])
                    else:
                        self.tt(ko[:, 0:N2], ko[:, 0:N2], X[:, 0:N2], ALU.add, [kok, Xk], [kok])
                        self.tt(ko[:, N2:2 * N2], ko[:, N2:2 * N2], X[:, N2:2 * N2], ALU.subtract, [kok, Xk], [kok])
                        self.stq(scr["KFD"][p, :, 0:2 * N2], ko[:, 0:2 * N2], [kok], [f"KFD{p}"])
                stages.append(comb)
                jobs.append(stages)
                jn += 1
        for p in range(npairs):
            b, pl = p // ppb, p % ppb
            q4, qi = p // 4, p % 4
            slot = jn % 2
            pre = []
            if pl == 0:
                pre.append((lambda b=b: (adv_z(10 ** 6), load_blk(scr["ZD"][:, toff:toff + Ls], 0, b, "ZD" + sn))))
            s1 = st_s1(blk[0][b % 2], f"blk0{b % 2}", pl)

            def first(pre=pre, s1=s1, p=p):
                for f in pre:
                    f()
                self.ld(kfi[p % 3][:, 0:2 * N2], scr["KFD"][p, :, 0:2 * N2], [f"KFD{p}"], [f"kfi{p % 3}"])
                s1()
            stages = [first, st_tw(slot), st_s3(slot)]

            def ymul(p=p, slot=slot):
                kt_, ktk = kfi[p % 3], f"kfi{p % 3}"
                yt_, ytk = Yt[slot], f"Yt{slot}"
                self.tt(ya[:, 0:N2], X[:, 0:N2], kt_[:, 0:N2], ALU.mult, [Xk, ktk], ["ya"])
                self.tt(ya[:, N2:2 * N2], X[:, N2:2 * N2], kt_[:, N2:2 * N2], ALU.mult, [Xk, ktk], ["ya"])
                self.tt(yb_[:, 0:N2], X[:, 0:N2], kt_[:, N2:2 * N2], ALU.mult, [Xk, ktk], ["yb"])
                self.tt(yb_[:, N2:2 * N2], X[:, N2:2 * N2], kt_[:, 0:N2], ALU.mult, [Xk, ktk], ["yb"])
                self.tt(yt_[:, 0:N2], ya[:, 0:N2], ya[:, N2:2 * N2], ALU.subtract, ["ya"], [ytk])
                self.tt(yt_[:, N2:2 * N2], yb_[:, 0:N2], yb_[:, N2:2 * N2], ALU.add, ["yb"], [ytk])
            stages.append(ymul)

            def is3(slot=slot):
                yt_, ytk = Yt[slot], f"Yt{slot}"
                for a in range(NA):
                    ysl_r = yt_[:, a * KA:(a + 1) * KA]
                    ysl_i = yt_[:, N2 + a * KA:N2 + (a + 1) * KA]
                    self.mm(HT[0:KA, a * 256:a * 256 + 128], ysl_r, T["BDC"], True, False, [ytk, "BD" + sn], [HTk])
                    self.mm(HT[0:KA, a * 256:a * 256 + 128], ysl_i, T["BDSn"], False, True, [ytk, "BD" + sn], [HTk])
                    self.mm(HT[0:KA, a * 256 + 128:a * 256 + 256], ysl_i, T["BDC"], True, False, [ytk, "BD" + sn], [HTk])
                    self.mm(HT[0:KA, a * 256 + 128:a * 256 + 256], ysl_r, T["BDS"], False, True, [ytk, "BD" + sn], [HTk])
            stages.append(is3)

            def itw(q4=q4, qi=qi):
                hp, hpk = Hp[q4 % 2], f"Hp{q4 % 2}"
                W_ = NA * 256
                self.tt(ha[0:KA, 0:W_], HT[0:KA, 0:W_], T["TI1"][0:KA, :], ALU.mult, [HTk, "TI" + sn], ["ha"])
                self.tt(hb_[0:KA, 0:W_], HT[0:KA, 0:W_], T["TI2"][0:KA, :], ALU.mult, [HTk, "TI" + sn], ["hb"])
                hav = ha[0:KA, 0:W_].rearrange("p (a n) -> p a n", a=NA)
                hbv = hb_[0:KA, 0:W_].rearrange("p (a n) -> p a n", a=NA)
                self.tt(hp[0:KA, 0, 0:NA, qi * 128:(qi + 1) * 128], hav[:, :, 0:128], hav[:, :, 128:256], ALU.add, ["ha"], [hpk])
                self.tt(hp[0:KA, 1, 0:NA, qi * 128:(qi + 1) * 128], hbv[:, :, 0:128], hbv[:, :, 128:256], ALU.add, ["hb"], [hpk])
            stages.append(itw)
            if qi == 3:
                def is1(q4=q4):
                    hp, hpk = Hp[q4 % 2], f"Hp{q4 % 2}"
                    for a in range(NA):
                        self.mm(G[0:K1, 0:512], T["IC"][0:KA, a, :], hp[0:KA, 0, a, :], a == 0, False, ["ICS" + sn, hpk], [Gk])
                        self.mm(G[0:K1, 0:512], T["ISn"][0:KA, a, :], hp[0:KA, 1, a, :], False, a == NA - 1, ["ICS" + sn, hpk], [Gk])

                def ycp(q4=q4):
                    ys, ysk = yst[q4 % 2], f"yst{q4 % 2}"
                    self.cp(ys[0:K1, :], G[0:K1, 0:512], [Gk], [ysk])
                    c0_ = q4 * 8
                    self.stq(scr["YC"][c0_:c0_ + 8, toff:toff + Ls].rearrange("c (n2 n1) -> n2 c n1", n1=64),
                             ys[0:K1, :].rearrange("p (c n) -> p c n", c=8), [ysk], ["YC"])
                tails[len(jobs)] = (is1, ycp)
            jobs.append(stages)
            jn += 1
        nst = 6
        for step in range(len(jobs) + nst + 1):
            for k in range(nst - 1, 0, -1):
                j = step - k
                if 0 <= j < len(jobs) and k < len(jobs[j]):
                    jobs[j][k]()
            if (step - nst) in tails:
                for f in tails[step - nst]:
                    f()
            if step < len(jobs):
                jobs[step][0]()
            if step % 3 == 0:
                adv_z(1)
            yield (1.0 if sname == "x" else 0.12)


def _genB(self, l, AR, S0, S1, Ob_):
    cfg, L, LK, P = self.cfg, self.L, self.LK, self.P
    scr = self.scr
    nha = cfg.nha
    update_ctx = (l < self.nlayers - 1)
    NKB = LK // 128
    KT = [AR.alloc([128, LK], BF16) for _ in range(2)]
    VA = [AR.alloc([128, NKB, 128], BF16) for _ in range(2)]
    QTc = [AR.alloc([128, 512], BF16) for _ in range(2)]
    gtc = [AR.alloc([128, 512], BF16) for _ in range(2)]
    NPT = 5
    pts = [AR.alloc([128, 512], BF16) for _ in range(NPT)]
    rden = AR.alloc([128, 512], F32)
    yo = [AR.alloc([128, 512], BF16) for _ in range(2)]
    for i in range(2):
        self.memset(VA[i][:, :, 64:128], 1.0, [f"VA{i}"])
    Stiles = [(S0[:, 0:512], "S0a"), (S0[:, 512:1024], "S0b"), (S1[:, 0:512], "S1a"), (S1[:, 512:1024], "S1b")]
    O, Ok = Ob_
    cnt = [0, 0, 0]
    LOOK = 3
    yield
    for h in range(nha):
        kt, ktk = KT[h % 2], f"KT{h % 2}"
        va, vak = VA[h % 2], f"VA{h % 2}"
        self.ld(kt[0:64, :], scr["KN"][h], writes=[ktk])
        self.ld(kt[64:96, :], scr["KR"], writes=[ktk])
        self.ld(va[:, :, 0:64], scr["VD"][:, h * 64:(h + 1) * 64].rearrange("(kb p) c -> p kb c", p=128), writes=[vak])
        LQ_ = self.LQ if (l == self.nlayers - 1) else L
        qchunks = [(c * 512, 512, 0, NKB) for c in range(LQ_ // 512)]
        if update_ctx:
            qchunks.append((L, CTX, NKB - 2, NKB))
        for (t0, n, kb0, kb1) in qchunks:
            qt, qtk = QTc[cnt[0] % 2], f"QTc{cnt[0] % 2}"
            gt, gtk = gtc[cnt[0] % 2], f"gtc{cnt[0] % 2}"
            yy, yk = yo[cnt[0] % 2], f"yo{cnt[0] % 2}"
            cnt[0] += 1
            self.ld(qt[0:96, 0:n], scr["QT"][h, :, t0:t0 + n], writes=[qtk])
            self.ld(gt[0:64, 0:n], scr["GT"][h * 64:(h + 1) * 64, t0:t0 + n], writes=[gtk])

            def emit_s(kb):
                S, Sk = Stiles[cnt[1] % 4]
                cnt[1] += 1
                self.mm(S[:, 0:n], kt[0:96, kb * 128:(kb + 1) * 128], qt[0:96, 0:n], True, True, [ktk, qtk], [Sk])
                pt, ptk = pts[cnt[2] % NPT], f"pt{cnt[2] % NPT}"
                cnt[2] += 1
                self.act(pt[:, 0:n], S[:, 0:n], AF.Exp, [Sk], [ptk])
                return pt, ptk

            def emit_pv(kb, pt, ptk):
                self.mm(O[:, 0:n], va[:, kb, :], pt[:, 0:n], kb == kb0, kb == kb1 - 1, [vak, ptk], [Ok])
            pend = []
            for kb in range(kb0, kb1):
                pend.append((kb,) + emit_s(kb))
                if len(pend) > LOOK:
                    emit_pv(*pend.pop(0))
                if (kb - kb0) % 2 == 1:
                    yield
            while pend:
                emit_pv(*pend.pop(0))
            self.act(rden[0:64, 0:n], O[64:128, 0:n], AF.Ln, [Ok], ["rden"])
            self.act(rden[0:64, 0:n], rden[0:64, 0:n], AF.Exp, ["rden"], ["rden"], scale=-1.0)
            self.tt(rden[0:64, 0:n], rden[0:64, 0:n], gt[0:64, 0:n], ALU.mult, ["rden", gtk], ["rden"])
            self.tt(yy[0:64, 0:n], O[0:64, 0:n], rden[0:64, 0:n], ALU.mult, [Ok, "rden"], [yk])
            self.stq(scr["YT"][h * 64:(h + 1) * 64, t0:t0 + n], yy[0:64, 0:n], [yk])


def _hy_d2_gen(self, H, tiles, sname, Ls, toff):
    cfg = self.cfg
    scr = self.scr
    nch = cfg.nch
    ncg = nch // 128
    cw, cb = H["cw"], H["cb"]
    ub, acc, zo = tiles
    chs = [(c0, min(512, Ls - c0)) for c0 in range(0, Ls, 512)]
    ci = 0
    for cg in range(ncg):
        for (c0, n) in chs:
            ubt, ubk = ub[ci % 2], f"ubz{ci % 2}"
            zt, ztk = zo[ci % 2], f"zoz{ci % 2}"
            ci += 1
            for pi_, part in enumerate([1, 2]):
                row = part * nch + cg * 128
                lo, hi_ = max(c0 - 1, 0), min(c0 + n + 1, Ls)
                if c0 == 0:
                    self.memset(ubt[:, pi_, 0:1], 0.0, [ubk])
                if c0 + n == Ls:
                    self.memset(ubt[:, pi_, n + 1:n + 2], 0.0, [ubk])
                self.ld(ubt[:, pi_, lo - (c0 - 1):hi_ - (c0 - 1)], scr["UT"][row:row + 128, toff + lo:toff + hi_], writes=[ubk])
                j = part * ncg + cg
                out, outk = acc[pi_], f"accz{pi_}"
                self.ts(out[:, 0:n], ubt[:, pi_, 0:n], cw[:, j, 0:1], ALU.mult, [ubk, "cw", "cb"], [outk], s2=cb[:, j:j + 1], op1=ALU.add)
                self.stt(out[:, 0:n], ubt[:, pi_, 1:n + 1], cw[:, j, 1:2], out[:, 0:n], ALU.mult, ALU.add, [ubk, "cw", outk], [outk])
                self.stt(out[:, 0:n], ubt[:, pi_, 2:n + 2], cw[:, j, 2:3], out[:, 0:n], ALU.mult, ALU.add, [ubk, "cw", outk], [outk])
            self.tt(zt[:, 0:n], acc[0][:, 0:n], acc[1][:, 0:n], ALU.mult, ["accz0", "accz1"], [ztk])
            self.stq(scr["ZD"][cg * 128:(cg + 1) * 128, toff + c0:toff + c0 + n], zt[:, 0:n], [ztk], ["ZD" + sname])
            yield


Builder.hy_d2_gen = _hy_d2_gen
Builder.hy_fft_gen = _hy_fft_gen
Builder.genB = _genB
```
